# Optimizing a Trainium2 kernel written in Bass

```python
import jax, jax.numpy as jnp
from jax import lax
import numpy as np

D_MODEL = 1024
BATCH = 8
SEQ = 2048
DEPTH = 1
DEC_BATCH = 128
DEC_SEQ = 8
PAST_LEN = 16384
PAGE_SIZE = 128

POOL_WIDTH = D_MODEL // 2
CONV_WIDTH = D_MODEL - POOL_WIDTH
POOL_WINDOWS = (2, 4, 8, 16)
POOL_GROUPS = len(POOL_WINDOWS)
POOL_GROUP_DIM = POOL_WIDTH // POOL_GROUPS
POOL_HIST = max(POOL_WINDOWS) - 1
CONV_K = 3
FFN_K = 3
D_FF = ((8 * D_MODEL // 3 + 127) // 128) * 128
IN_WIDTH = POOL_WIDTH + 3 * CONV_WIDTH
EPS = 1e-6

kernel_name = "hybrid_pool_shortconv_convffn_step"


def rmsnorm(x, g):
    xf = x.astype(jnp.float32)
    y = xf * lax.rsqrt(jnp.mean(xf * xf, axis=-1, keepdims=True) + EPS)
    return (y * g.astype(jnp.float32)).astype(x.dtype)


def causal_dwconv(v_ext, w):
    k = w.shape[0]
    t = v_ext.shape[1] - (k - 1)
    out = w[0] * v_ext[:, 0:t]
    for j in range(1, k):
        out = out + w[j] * v_ext[:, j:j + t]
    return out


def pool_mix(u_ext, pos, w_grp, scale):
    n, l, c = u_ext.shape
    t = l - POOL_HIST
    uf = u_ext.astype(jnp.float32)
    cs = jnp.concatenate([jnp.zeros((n, 1, c), jnp.float32), jnp.cumsum(uf, axis=1)], axis=1)
    u_cur = uf[:, POOL_HIST:]
    posf = pos.astype(jnp.float32)
    diffs = []
    for g, win in enumerate(POOL_WINDOWS):
        sl = slice(g * POOL_GROUP_DIM, (g + 1) * POOL_GROUP_DIM)
        s = cs[:, POOL_HIST + 1:POOL_HIST + 1 + t, sl] - cs[:, POOL_HIST + 1 - win:POOL_HIST + 1 - win + t, sl]
        cnt = jnp.minimum(posf + 1.0, float(win))[None, :, None]
        diffs.append(s / cnt - u_cur[:, :, sl])
    d = jnp.stack(diffs, axis=2)
    y = jnp.einsum('ntgc,gcd->ntgd', d, w_grp.astype(jnp.float32)).reshape(n, t, c)
    return (y * scale.astype(jnp.float32)).astype(u_ext.dtype)


def layer(x, pos, st_pool, st_conv, st_ffn, g1, w_in, w_pool_grp, pool_scale, conv_w, w_out, g2, w_up, ffn_conv_w, w_down):
    h = rmsnorm(x, g1)
    z = jnp.einsum('ntd,de->nte', h, w_in)
    u = z[..., :POOL_WIDTH]
    gb = z[..., POOL_WIDTH:POOL_WIDTH + CONV_WIDTH]
    gc = z[..., POOL_WIDTH + CONV_WIDTH:POOL_WIDTH + 2 * CONV_WIDTH]
    hv = z[..., POOL_WIDTH + 2 * CONV_WIDTH:]
    u_ext = jnp.concatenate([st_pool.astype(u.dtype), u], axis=1)
    y_pool = pool_mix(u_ext, pos, w_pool_grp, pool_scale)
    new_pool = u_ext[:, -POOL_HIST:]
    vc_ext = jnp.concatenate([st_conv.astype(u.dtype), gc * hv], axis=1)
    y_conv = gb * causal_dwconv(vc_ext, conv_w)
    new_conv = vc_ext[:, -(CONV_K - 1):]
    x = x + jnp.einsum('nte,ed->ntd', jnp.concatenate([y_pool, y_conv], axis=-1), w_out)
    h2 = rmsnorm(x, g2)
    up = jnp.einsum('ntd,df->ntf', h2, w_up)
    up_ext = jnp.concatenate([st_ffn.astype(up.dtype), up], axis=1)
    upc = causal_dwconv(up_ext, ffn_conv_w)
    act = jax.nn.silu(upc[..., :D_FF]) * upc[..., D_FF:]
    x = x + jnp.einsum('ntf,fd->ntd', act, w_down)
    new_ffn = up_ext[:, -(FFN_K - 1):]
    return x, new_pool, new_conv, new_ffn


def setup_inputs(seed: int = 0) -> dict:
    key = jax.random.key(seed)
    ks = jax.random.split(key, 20)
    f32 = jnp.float32
    nrm = lambda k, s, sc: jax.random.normal(k, s, f32) * sc
    return {
        "x_prompt": nrm(ks[0], (BATCH, SEQ, D_MODEL), 1.0),
        "x_sample": nrm(ks[1], (DEC_BATCH, DEC_SEQ, D_MODEL), 1.0),
        "state_pool": nrm(ks[2], (DEPTH, DEC_BATCH, POOL_HIST, POOL_WIDTH), 1.0),
        "state_conv": nrm(ks[3], (DEPTH, DEC_BATCH, CONV_K - 1, CONV_WIDTH), 1.0),
        "state_ffn": nrm(ks[4], (DEPTH, DEC_BATCH, FFN_K - 1, 2 * D_FF), 1.0),
        "norm1_g": 1.0 + nrm(ks[5], (DEPTH, D_MODEL), 0.02),
        "w_in": nrm(ks[6], (DEPTH, D_MODEL, IN_WIDTH), D_MODEL ** -0.5),
        "w_pool_grp": nrm(ks[7], (DEPTH, POOL_GROUPS, POOL_GROUP_DIM, POOL_GROUP_DIM), POOL_GROUP_DIM ** -0.5),
        "pool_scale": 1.0 + nrm(ks[8], (DEPTH, POOL_WIDTH), 0.02),
        "conv_w": nrm(ks[9], (DEPTH, CONV_K, CONV_WIDTH), CONV_K ** -0.5),
        "w_out": nrm(ks[10], (DEPTH, D_MODEL, D_MODEL), D_MODEL ** -0.5),
        "norm2_g": 1.0 + nrm(ks[11], (DEPTH, D_MODEL), 0.02),
        "w_up": nrm(ks[12], (DEPTH, D_MODEL, 2 * D_FF), D_MODEL ** -0.5),
        "ffn_conv_w": nrm(ks[13], (DEPTH, FFN_K, 2 * D_FF), FFN_K ** -0.5),
        "w_down": nrm(ks[14], (DEPTH, D_FF, D_MODEL), D_FF ** -0.5),
        "final_g": 1.0 + nrm(ks[15], (D_MODEL,), 0.02),
    }


def reference(x_prompt, x_sample, state_pool, state_conv, state_ffn, norm1_g, w_in, w_pool_grp, pool_scale, conv_w, w_out, norm2_g, w_up, ffn_conv_w, w_down, final_g):
    bp = x_prompt.shape[0]
    pos_p = jnp.arange(x_prompt.shape[1], dtype=jnp.int32)
    pos_s = PAST_LEN + jnp.arange(x_sample.shape[1], dtype=jnp.int32)
    xp, xs = x_prompt, x_sample
    pp, cp, fp_, ps, cs_, fs = [], [], [], [], [], []
    for l in range(DEPTH):
        wl = (norm1_g[l], w_in[l], w_pool_grp[l], pool_scale[l], conv_w[l], w_out[l], norm2_g[l], w_up[l], ffn_conv_w[l], w_down[l])
        z_pool = jnp.zeros((bp, POOL_HIST, POOL_WIDTH), xp.dtype)
        z_conv = jnp.zeros((bp, CONV_K - 1, CONV_WIDTH), xp.dtype)
        z_ffn = jnp.zeros((bp, FFN_K - 1, 2 * D_FF), xp.dtype)
        xp, a, b, c = layer(xp, pos_p, z_pool, z_conv, z_ffn, *wl)
        pp.append(a); cp.append(b); fp_.append(c)
        xs, a, b, c = layer(xs, pos_s, state_pool[l], state_conv[l], state_ffn[l], *wl)
        ps.append(a); cs_.append(b); fs.append(c)
    y_prompt = rmsnorm(xp, final_g)
    y_sample = rmsnorm(xs, final_g)
    return (y_prompt, y_sample, jnp.stack(pp), jnp.stack(cp), jnp.stack(fp_), jnp.stack(ps), jnp.stack(cs_), jnp.stack(fs))
```

```python
import numpy as np
from contextlib import ExitStack

import concourse.bass as bass
import concourse.mybir as mybir
from concourse.bass_utils import run_bass_kernel_spmd

F32 = mybir.dt.float32
BF16 = mybir.dt.bfloat16
AF = mybir.ActivationFunctionType
ALU = mybir.AluOpType

D = 1024
SEQ = 2048
NCORES = 8
DEC_B = 128
DEC_T = 8
SPC = DEC_B // NCORES
PW = 512
DFF = 2816
NPAIR = DFF // 128
WINS = (2, 4, 8, 16)
EPS = 1e-6
NTILE = 17
GROUP = 3
RING_UP = GROUP + 1
RING_DN = 2 * GROUP
NSCR = 10
SCRW = 528

ENGS = ("pe", "act", "dve", "pool", "sp")
STRICT_SAME_ENGINE = False


class Buf:
    def __init__(self, name):
        self.name = name
        self.last_w = None
        self.readers = []
        self.aliases = []


class Op:
    __slots__ = ("eng", "fn", "reads", "writes", "is_dma", "key", "idx", "pos", "waits",
                 "signal", "sigidx", "clock", "dma_sem", "dma_val", "name", "tiny")

    def __init__(self, eng, fn, reads, writes, is_dma, key, idx, name):
        self.eng = eng
        self.fn = fn
        self.reads = reads
        self.writes = writes
        self.is_dma = is_dma
        self.key = key
        self.idx = idx
        self.name = name
        self.waits = []
        self.signal = False
        self.sigidx = None
        self.clock = None
        self.dma_sem = None
        self.dma_val = None
        self.pos = None


class Prog:
    def __init__(self):
        self.ops = []
        self.t = 0.0

    def op(self, eng, fn, reads=(), writes=(), dma=False, lag=0.0, name="", tiny=False):
        o = Op(eng, fn, list(reads), list(writes), dma, self.t + lag, len(self.ops), name)
        o.tiny = tiny
        self.ops.append(o)
        return o

    def schedule(self, dma_k):
        ops = sorted(self.ops, key=lambda o: (o.key, o.idx))
        streams = {e: [] for e in ENGS}
        seen = {e: {f: -1 for f in ENGS} for e in ENGS}
        known_dma = {e: set() for e in ENGS}
        dma_hist = {e: [] for e in ENGS}
        for o in ops:
            o.pos = len(streams[o.eng])
            streams[o.eng].append(o)
            deps = []
            for b0 in o.reads:
                for b in [b0] + b0.aliases:
                    if b.last_w is not None:
                        deps.append(b.last_w)
            for b0 in o.writes:
                for b in [b0] + b0.aliases:
                    if b.last_w is not None:
                        deps.append(b.last_w)
                    deps.extend(b.readers)
            rd = o.reads
            wr = o.writes
            if o.is_dma:
                h = dma_hist[o.eng]
                kq = dma_k[o.eng]
                if len(h) >= kq:
                    deps.append(h[len(h) - kq])
                n = len(h)
                o.dma_sem = n % kq
                o.dma_val = 16 * (n // kq + 1)
                h.append(o)
            for b in rd:
                b.readers.append(o)
            for b in wr:
                b.last_w = o
                b.readers = []
            best = {}
            for d in deps:
                if d is o:
                    continue
                if d.is_dma:
                    if d not in known_dma[o.eng]:
                        known_dma[o.eng].add(d)
                        o.waits.append(d)
                else:
                    if d.eng == o.eng and not o.is_dma:
                        if (STRICT_SAME_ENGINE or (d.tiny and o.pos - d.pos <= 8)) and seen[o.eng][o.eng] < d.pos:
                            if d.eng not in best or best[d.eng].pos < d.pos:
                                best[d.eng] = d
                        continue
                    if seen[o.eng][d.eng] >= d.pos:
                        continue
                    if d.eng not in best or best[d.eng].pos < d.pos:
                        best[d.eng] = d
            for f, d in best.items():
                if seen[o.eng][f] >= d.pos:
                    continue
                d.signal = True
                o.waits.append(d)
                for g2, p in d.clock.items():
                    if seen[o.eng][g2] < p:
                        seen[o.eng][g2] = p
            if not o.is_dma:
                c = dict(seen[o.eng])
                c[o.eng] = o.pos
                o.clock = c
        for e in ENGS:
            n = 0
            for o in streams[e]:
                if o.signal:
                    n += 1
                    o.sigidx = n
        self.streams = streams
        self.dma_hist = dma_hist
        return streams


def build_program():
    nc = bass.Bass("TRN2", target_bir_lowering=False)
    P = Prog()

    def dram(name, shape, kind):
        return nc.dram_tensor(name, list(shape), F32, kind=kind).ap()

    x_d = dram("x", [NTILE * 128, D], "ExternalInput")
    spT_d = dram("spT", [128, 4 * SPC * 15], "ExternalInput")
    scT_d = dram("scT", [128, 4 * SPC * 2], "ExternalInput")
    sfT_d = dram("sfT", [128, 44 * SPC * 2], "ExternalInput")
    g1_d = dram("g1", [D], "ExternalInput")
    g2_d = dram("g2", [D], "ExternalInput")
    gf_d = dram("gf", [D], "ExternalInput")
    win_d = dram("w_in", [D, 2048], "ExternalInput")
    wpool_d = dram("w_pool", [4, 128, 128], "ExternalInput")
    wout_d = dram("w_out", [D, D], "ExternalInput")
    wup_d = dram("w_up", [D, 2 * DFF], "ExternalInput")
    wdn_d = dram("w_down", [DFF, D], "ExternalInput")
    small_d = dram("small", [128, 4 + 12 + 132 + 60], "ExternalInput")
    ident_d = dram("ident", [128, 128], "ExternalInput")

    y_d = dram("y", [NTILE * 128, D], "ExternalOutput")
    npp_d = dram("npp", [128, 4 * 15], "ExternalOutput")
    ncp_d = dram("ncp", [128, 4 * 2], "ExternalOutput")
    nfp_d = dram("nfp", [128, NPAIR * 2 * 2], "ExternalOutput")
    nps_d = dram("nps", [128, 4 * SPC * 15], "ExternalOutput")
    ncs_d = dram("ncs", [128, 4 * SPC * 2], "ExternalOutput")
    nfs_d = dram("nfs", [128, 44 * SPC * 2], "ExternalOutput")

    es = ExitStack()
    with es:
        def sb(name, shape, dt):
            return es.enter_context(nc.sbuf_tensor("sb_" + name, list(shape), dt))

        x1 = sb("x1", [128, 9, D], F32)
        h2T = sb("h2T", [128, 8, 1152], BF16)
        g1r = sb("g1r", [128, D], F32)
        g2r = sb("g2r", [128, D], F32)
        gfr = sb("gfr", [128, D], F32)
        small = sb("small", [128, 208], F32)
        mhalf = sb("mhalf", [128, 1], F32)
        epsb = sb("epsb", [128, 1], F32)
        stt = sb("stt", [128, 48], F32)
        w_in = sb("w_in_s", [128, 8, 2048], BF16)
        w_out = sb("w_out_s", [128, 8, D], BF16)
        w_pool = sb("w_pool_s", [128, 4, 128], BF16)
        ident = sb("ident", [128, 128], BF16)
        identF = sb("identF", [128, 128], F32)
        hb = [sb(f"hb{i}", [128, D], BF16) for i in range(2)]
        hT = sb("hT", [128, 8, 512], BF16)
        ycat = sb("ycat", [128, 8, 512], BF16)
        scr = [sb(f"scr{i}", [128, SCRW], F32) for i in range(NSCR)]
        dT = [sb(f"dT{i}", [128, 512], BF16) for i in range(2)]
        Hu = sb("Hu", [128, 4, 15], F32)
        Hv = sb("Hv", [128, 4, 2], F32)
        Hf = sb("Hf", [128, NPAIR, 2, 2], F32)
        ues = sb("ues", [128, 4, 23 * SPC], F32)
        ves = sb("ves", [128, 4, 10 * SPC], F32)
        hfs = sb("hfs", [128, 44, 2 * SPC], F32)
        wup_r = [sb(f"wup{i}", [128, 2, 8, 128], BF16) for i in range(RING_UP)]
        assert RING_DN == 6 and GROUP == 3
        arena = sb("arena", [128, RING_DN * D + 2 * GROUP * 512], BF16)
        wdn_r = [arena[:, i * D:(i + 1) * D] for i in range(RING_DN)]
        actb = [arena[:, RING_DN * D + i * 512:RING_DN * D + (i + 1) * 512] for i in range(2 * GROUP)]
        hT2 = arena[:, 0:4096].rearrange("p (c t) -> p c t", c=8)
        ycat2 = arena[:, 4096:8192].rearrange("p (c t) -> p c t", c=8)
        HT = [hT, hT2]
        YC = [ycat, ycat2]
        hbA = arena[:, 8192:9216]
        banks = [es.enter_context(nc.psum_tensor(f"bank{i}", [128, 512], F32)) for i in range(8)]

        pscale = small[:, 0:4]

        def cw(k, c):
            return small[:, 4 + k * 4 + c: 4 + k * 4 + c + 1]

        def fcw(k, ch):
            return small[:, 16 + k * 44 + ch: 16 + k * 44 + ch + 1]

        def invc(c):
            return small[:, 148 + c * 15: 148 + (c + 1) * 15]

        B_x1 = [Buf(f"x1_{i}") for i in range(9)]
        B_h2T = [Buf(f"h2T_{i}") for i in range(9)]
        B_g1, B_g2, B_gf, B_small, B_mhalf = Buf("g1"), Buf("g2"), Buf("gf"), Buf("small"), Buf("mhalf")
        B_st = [Buf(f"st{i}") for i in range(16)]
        B_wout, B_wpool, B_ident = Buf("wout"), Buf("wpool"), Buf("ident")
        B_identF = Buf("identF")
        B_winq = [Buf(f"win{q}") for q in range(4)]
        B_hb = [Buf("hb0"), Buf("hb1")]
        B_hT = [[Buf(f"hT{p_}_{i}") for i in range(4)] for p_ in range(2)]
        B_ycat = [[Buf(f"ycat{p_}_{i}") for i in range(8)] for p_ in range(2)]
        B_scr = [Buf(f"scr{i}") for i in range(NSCR)]
        B_dT = [Buf("dT0"), Buf("dT1")]
        B_Hu = [Buf(f"Hu{i}") for i in range(4)]
        B_Hv = [Buf(f"Hv{i}") for i in range(4)]
        B_Hf = [Buf(f"Hf{i}") for i in range(NPAIR)]
        B_ues = [Buf(f"ues{i}") for i in range(4)]
        B_ves = [Buf(f"ves{i}") for i in range(4)]
        B_hfs = [Buf(f"hfs{i}") for i in range(NPAIR)]
        B_wup = [[Buf(f"wup{i}a"), Buf(f"wup{i}b")] for i in range(RING_UP)]
        B_wdn = [Buf(f"wdn{i}") for i in range(RING_DN)]
        B_act = [Buf(f"act{i}") for i in range(2 * GROUP)]

        def alias(a_, b_):
            a_.aliases.append(b_)
            b_.aliases.append(a_)
        for t_ in range(4):
            for i_ in range(4):
                alias(B_hT[1][t_], B_wdn[i_])
        for e_ in range(8):
            alias(B_ycat[1][e_], B_wdn[4 + e_ // 2] if e_ < 4 else B_act[e_ - 4])
        B_hbA = Buf("hbA")
        alias(B_hbA, B_act[4])
        alias(B_hbA, B_act[5])
        hbB = ycat[:, 2:4, :].rearrange("p c t -> p (c t)")
        hbC = ycat[:, 4:6, :].rearrange("p c t -> p (c t)")
        B_hbB, B_hbC = Buf("hbB"), Buf("hbC")
        alias(B_hbB, B_ycat[0][2])
        alias(B_hbB, B_ycat[0][3])
        alias(B_hbC, B_ycat[0][4])
        alias(B_hbC, B_ycat[0][5])
        B_bank = [Buf(f"bank{i}") for i in range(8)]
        B_out = Buf("outs")

        cnt = {"bank": 0, "scr": 0, "hb": 0, "st": 0, "dT": 0, "act": 0, "bank_up": 0, "bank_dn": 0}

        def nxt(kind, n):
            v = cnt[kind] % n
            cnt[kind] += 1
            return v

        def nbank():
            k = nxt("bank", 8)
            return banks[k], B_bank[k]

        def nbank_up():
            k = nxt("bank_up", 4)
            return banks[k], B_bank[k]

        def nbank_dn():
            k = 4 + nxt("bank_dn", 4)
            return banks[k], B_bank[k]

        def nscr():
            k = nxt("scr", NSCR)
            return scr[k], B_scr[k]

        late = []

        def dma(eng, out, in_, reads, writes, name="", lag=0.0):
            P.op(eng, lambda e, out=out, in_=in_: e.dma_start(out=out, in_=in_), reads, writes, dma=True, name=name, lag=lag)

        def bcast(v):
            return v.rearrange("(o d) -> o d", o=1).to_broadcast([128, D])

        def load_x(part_tile, glob_tile, lag=0.0):
            dma("sp", x1[:, part_tile, :], x_d[glob_tile * 128:(glob_tile + 1) * 128, :], [], [B_x1[part_tile]],
                f"ld_x{glob_tile}", lag=lag)

        load_x(0, 0)
        dma("sp", g1r[:], bcast(g1_d), [], [B_g1], "ld_g1")
        for i in range(1, 4):
            load_x(i, i)
        dma("sp", small[:], small_d[:, :], [], [B_small], "ld_small")
        dma("sp", g2r[:], bcast(g2_d), [], [B_g2], "ld_g2")
        def late_loads():
            gate = [B_winq[1]]
            for i in range(4, 8):
                dma("sp", x1[:, i, :], x_d[i * 128:(i + 1) * 128, :], gate, [B_x1[i]], f"ld_x{i}")
            dma("sp", gfr[:], bcast(gf_d), gate, [B_gf], "ld_gf")
            dma("sp", identF[:], ident_d[:, :], gate, [B_identF], "ld_identF")
            dma("sp", ues[:, :, 0:15 * SPC], spT_d.rearrange("p (c q) -> p c q", c=4), gate, B_ues, "ld_sp")
            dma("sp", ves[:, :, 0:2 * SPC], scT_d.rearrange("p (c q) -> p c q", c=4), gate, B_ves, "ld_sc")
            dma("sp", hfs[:], sfT_d.rearrange("p (c q) -> p c q", c=44), gate, B_hfs, "ld_sf")

        dma("pool", ident[:], ident_d[:, :], [], [B_ident], "ld_ident")
        def load_win(q):
            gate_ = [B_x1[1]] if q == 0 else []
            dma("pool", w_in[:, :, q * 512:(q + 1) * 512],
                win_d.rearrange("(c p) e -> p c e", p=128)[:, :, q * 512:(q + 1) * 512], gate_, [B_winq[q]], f"ld_win{q}")

        def load_rest_of_mixer_weights():
            for q in (2, 3, 1):
                load_win(q)
            dma("pool", w_pool[:], wpool_d.rearrange("g c d -> c g d"), [], [B_wpool], "ld_wpool")
            dma("pool", w_out[:], wout_d.rearrange("(c p) e -> p c e", p=128), [], [B_wout], "ld_wout")
        load_win(0)

        P.op("pool", lambda e: e.memset(mhalf[:], -0.5), [], [B_mhalf], name="mhalf")
        P.op("pool", lambda e: e.memset(epsb[:], EPS), [], [B_mhalf], name="epsb")
        P.op("act", lambda e: e.activation(out=stt[:, 47:48], in_=epsb[:], func=AF.Sqrt), [B_mhalf], [Buf("warm")], name="warm")
        P.op("pool", lambda e: e.memset(Hu[:], 0.0), [], B_Hu, name="Hu0")
        P.op("pool", lambda e: e.memset(Hv[:], 0.0), [], B_Hv, name="Hv0")
        P.op("pool", lambda e: e.memset(Hf[:], 0.0), [], B_Hf, name="Hf0")

        up_slot = {}
        dn_slot = {}
        all_pairs = [(0, jj) for jj in range(NPAIR)] + [(1, jj) for jj in range(NPAIR)]
        pend_up = list(all_pairs)
        pend_dn = list(all_pairs)
        ring_n = {"up": 0, "dn": 0}

        def load_up(extra_reads=()):
            if not pend_up:
                return
            part, jj = pend_up.pop(0)
            s_ = ring_n["up"] % RING_UP
            ring_n["up"] += 1
            up_slot[(part, jj)] = s_
            for ab in range(2):
                f0 = ab * DFF + jj * 128
                src = wup_d.rearrange("(c p) f -> p c f", p=128)[:, :, f0:f0 + 128]
                dma("pool", wup_r[s_][:, ab, :, :], src, list(extra_reads), [B_wup[s_][ab]], f"ld_wup{part}_{jj}_{ab}")

        def load_dn(extra_reads=(), only_part=None):
            if not pend_dn:
                return
            if only_part is not None and pend_dn[0][0] != only_part:
                return
            part, jj = pend_dn.pop(0)
            s_ = ring_n["dn"] % RING_DN
            ring_n["dn"] += 1
            dn_slot[(part, jj)] = s_
            dma("pool", wdn_r[s_], wdn_d[jj * 128:(jj + 1) * 128, :], list(extra_reads), [B_wdn[s_]], f"ld_wdn{part}_{jj}")

        def rms_to_T(src_ap, B_src, grep, B_g, dstT, dst_cols, B_dst, lag=0.0, defer=False, fixed_hb=None):
            state = {}

            def front():
                if fixed_hb is None:
                    k = nxt("hb", 2)
                    hbk, Bhbk = hb[k], B_hb[k]
                else:
                    hbk, Bhbk = fixed_hb
                s = nxt("st", 16)
                c0 = s * 3
                state["k"] = (hbk, Bhbk)
                P.op("act", lambda e: e.activation(out=hbk[:, :], in_=src_ap, func=AF.Square, accum_out=stt[:, c0:c0 + 1]),
                     [B_src], [Bhbk, B_st[s]], name="sq", lag=lag, tiny=True)
                P.op("act", lambda e: e.activation(out=stt[:, c0 + 1:c0 + 2], in_=stt[:, c0:c0 + 1], func=AF.Sqrt,
                                                   scale=1.0 / D, bias=epsb[:]),
                     [B_st[s], B_mhalf], [B_st[s]], name="ms", lag=lag, tiny=True)
                P.op("dve", lambda e: e.reciprocal(out=stt[:, c0 + 2:c0 + 3], in_=stt[:, c0 + 1:c0 + 2]),
                     [B_st[s]], [B_st[s]], name="pow", lag=lag, tiny=True)
                P.op("dve", lambda e: e.scalar_tensor_tensor(out=hbk[:, :], in0=src_ap, scalar=stt[:, c0 + 2:c0 + 3], in1=grep[:],
                                                             op0=ALU.mult, op1=ALU.mult),
                     [B_src, B_st[s], B_g], [Bhbk], name="hnorm", lag=lag)

            def back():
                hbk, Bhbk = state["k"]
                bk, Bbk = nbank()
                pT = bk.bitcast(BF16)

                def tr(e):
                    ins = None
                    for c in range(8):
                        ins = e.transpose(out=pT[:, c * 128:(c + 1) * 128], in_=hbk[:, c * 128:(c + 1) * 128], identity=ident[:])
                    return ins
                P.op("pe", tr, [Bhbk, B_ident], [Bbk], name="tr", lag=lag)
                P.op("act", lambda e: e.activation(out=dstT[:, :, dst_cols[0]:dst_cols[1]],
                                                   in_=pT[:, :].rearrange("p (c t) -> p c t", c=8), func=AF.Copy),
                     [Bbk], [B_dst], name="trcp", lag=lag)
            if defer == "both":
                return front, back
            front()
            if defer:
                return back
            back()
            return None

        def stage_A(tiles):
            backs = []
            for t, i in enumerate(tiles):
                backs.append(rms_to_T(x1[:, i, :], B_x1[i], g1r, B_g1, HT[0], (t * 128, (t + 1) * 128), B_hT[0][t], defer=True))
                if len(backs) >= 2:
                    backs.pop(0)()
            for bfn in backs:
                bfn()

        def mixer_block(part, tiles, is_sample, first_prompt, next_tiles, par, next_par, before_nA=None, early_A=True):
            nt = len(tiles)
            NT = nt * 128
            hTb, ycb = HT[par], YC[par]
            Byc = B_ycat[par]
            hT_bufs = [B_hT[par][t] for t in range(nt)]

            def inproj(bk, echunk):
                def f(e):
                    ins = None
                    for d in range(8):
                        ins = e.matmul(bk[:, 0:NT], lhsT=w_in[:, d, echunk * 128:(echunk + 1) * 128], rhs=hTb[:, d, 0:NT],
                                       start=(d == 0), stop=(d == 7))
                    return ins
                return f

            TS = SPC if is_sample else 1
            HU = 15 * TS
            HV = 2 * TS

            def bu_front(c):
                win = WINS[c]
                bk, Bbk = nbank()
                P.op("pe", inproj(bk, c), hT_bufs + [B_winq[0]], [Bbk], name=f"inproj_u{c}")
                L = HU + NT
                if is_sample:
                    ue_flat = ues[:, c, :]
                    Bue = B_ues[c]
                else:
                    ue_flat, Bue = nscr()
                    P.op("dve", lambda e, su=ue_flat, c=c: e.tensor_copy(out=su[:, 0:15], in_=Hu[:, c, :]), [B_Hu[c]], [Bue],
                         name="halo_u", tiny=True)
                P.op("act", lambda e, bk=bk, su=ue_flat: e.activation(out=su[:, HU:L], in_=bk[:, 0:NT], func=AF.Copy),
                     [Bbk], [Bue], name="evac_u")
                tA, BtA = nscr()
                tB, BtB = nscr()
                cur, Bcur = ue_flat, Bue
                tmp = [(tA, BtA), (tB, BtB)]
                w = 1
                step = 0
                while w < win:
                    lo = (15 - (win - 2 * w)) * TS
                    sh = w * TS
                    dst, Bdst = tmp[step % 2]
                    P.op("dve", lambda e, dst=dst, cur=cur, lo=lo, sh=sh, L=L: e.tensor_tensor(
                        out=dst[:, lo:L], in0=cur[:, lo:L], in1=cur[:, lo - sh:L - sh], op=ALU.add),
                        [Bcur], [Bdst], name=f"pool_s{2 * w}")
                    cur, Bcur = dst, Bdst
                    w *= 2
                    step += 1
                kd = nxt("dT", 2)
                P.op("dve", lambda e, cur=cur, ue_flat=ue_flat, kd=kd, win=win: e.scalar_tensor_tensor(
                    out=dT[kd][:, 0:NT], in0=cur[:, HU:L], scalar=1.0 / win, in1=ue_flat[:, HU:L],
                    op0=ALU.mult, op1=ALU.subtract), [Bcur, Bue], [B_dT[kd]], name="pool_d")
                if first_prompt:
                    other, Bother = tmp[step % 2]
                    P.op("pool", lambda e, other=other, cur=cur, c=c: e.tensor_tensor(
                        out=other[:, 0:15], in0=cur[:, 15:30], in1=invc(c), op=ALU.mult),
                        [Bcur, B_small], [Bother], name="pool_fix1", tiny=True)
                    P.op("dve", lambda e, other=other, ue_flat=ue_flat, kd=kd: e.tensor_tensor(
                        out=dT[kd][:, 0:15], in0=other[:, 0:15], in1=ue_flat[:, 15:30], op=ALU.subtract),
                        [Bother, Bue], [B_dT[kd]], name="pool_fix2", tiny=True)
                if not is_sample:
                    P.op("dve", lambda e, ue_flat=ue_flat, c=c: e.tensor_copy(out=Hu[:, c, :], in_=ue_flat[:, NT:NT + 15]),
                         [Bue], [B_Hu[c]], name="halo_u_save", tiny=True)
                return kd

            def bu_back(c, kd):
                bk2, Bbk2 = nbank()
                P.op("pe", lambda e, bk2=bk2, kd=kd, c=c: e.matmul(bk2[:, 0:NT], lhsT=w_pool[:, c, :], rhs=dT[kd][:, 0:NT],
                                                                   start=True, stop=True),
                     [B_dT[kd], B_wpool], [Bbk2], name="poolmm")
                P.op("act", lambda e, bk2=bk2, c=c: e.activation(out=ycb[:, c, 0:NT], in_=bk2[:, 0:NT], func=AF.Identity,
                                                                 scale=pscale[:, c:c + 1]),
                     [Bbk2, B_small], [Byc[c]], name="evac_ypool")

            def bc(c):
                bgc, Bgc = nbank()
                bhv, Bhv = nbank()
                bgb, Bgb = nbank()
                P.op("pe", inproj(bgc, 8 + c), hT_bufs + [B_winq[2]], [Bgc], name=f"inproj_gc{c}")
                P.op("pe", inproj(bhv, 12 + c), hT_bufs + [B_winq[3]], [Bhv], name=f"inproj_hv{c}")
                P.op("pe", inproj(bgb, 4 + c), hT_bufs + [B_winq[1]], [Bgb], name=f"inproj_gb{c}")
                sg, Bsg = nscr()
                P.op("act", lambda e, sg=sg, bgc=bgc: e.activation(out=sg[:, 0:NT], in_=bgc[:, 0:NT], func=AF.Copy),
                     [Bgc], [Bsg], name="evac_gc")
                acc, Bacc = nscr()
                if is_sample:
                    sv = ves[:, c, :]
                    Bsv = B_ves[c]
                else:
                    sv, Bsv = nscr()
                    P.op("dve", lambda e, sv=sv, c=c: e.tensor_copy(out=sv[:, 0:2], in_=Hv[:, c, :]), [B_Hv[c]], [Bsv],
                         name="halo_v", tiny=True)
                P.op("dve", lambda e, sv=sv, bhv=bhv, sg=sg: e.tensor_tensor(
                    out=sv[:, HV:HV + NT], in0=bhv[:, 0:NT], in1=sg[:, 0:NT], op=ALU.mult), [Bhv, Bsg], [Bsv], name="v")
                P.op("dve", lambda e, sv=sv, acc=acc, c=c: e.tensor_scalar(
                    out=acc[:, 0:NT], in0=sv[:, HV:HV + NT], scalar1=cw(2, c), scalar2=None, op0=ALU.mult),
                    [Bsv, B_small], [Bacc], name="cv2")
                for k in (1, 0):
                    P.op("dve", lambda e, sv=sv, acc=acc, k=k, c=c: e.scalar_tensor_tensor(
                        out=acc[:, 0:NT], in0=sv[:, k * TS:k * TS + NT], scalar=cw(k, c), in1=acc[:, 0:NT], op0=ALU.mult, op1=ALU.add),
                        [Bsv, B_small, Bacc], [Bacc], name=f"cv{k}")
                if not is_sample:
                    P.op("dve", lambda e, sv=sv, c=c: e.tensor_copy(out=Hv[:, c, :], in_=sv[:, NT:NT + 2]),
                         [Bsv], [B_Hv[c]], name="halo_v_save", tiny=True)
                P.op("dve", lambda e, bgb=bgb, acc=acc, c=c: e.tensor_tensor(
                    out=ycb[:, 4 + c, 0:NT], in0=bgb[:, 0:NT], in1=acc[:, 0:NT], op=ALU.mult),
                    [Bgb, Bacc], [Byc[4 + c]], name="yconv")

            def genB():
                nA = []
                if next_tiles is not None:
                    nA = [rms_to_T(x1[:, i, :], B_x1[i], g1r, B_g1, HT[next_par], (t * 128, (t + 1) * 128), B_hT[next_par][t],
                                   defer="both", fixed_hb=(hbA, B_hbA) if early_A else None) for t, i in enumerate(next_tiles)]
                asteps = []
                if early_A:
                    for t in range(len(nA)):
                        asteps.append(nA[t][0])
                        asteps.append(nA[t][1])

                def a_step(n=1):
                    for _ in range(n):
                        if asteps:
                            asteps.pop(0)()
                k0 = bu_front(0)
                yield
                k1 = bu_front(1)
                a_step()
                yield
                bc(0)
                yield
                bu_back(0, k0)
                bu_back(1, k1)
                a_step()
                yield
                k2 = bu_front(2)
                a_step()
                yield
                k3 = bu_front(3)
                yield
                bc(1)
                a_step()
                yield
                bu_back(2, k2)
                bu_back(3, k3)
                a_step()
                yield
                if not early_A and nA:
                    for t in range(min(2, len(nA))):
                        nA[t][0]()
                bc(2)
                a_step()
                yield
                a_step()
                bc(3)
                a_step()
                yield
                a_step(8)
                if not early_A and nA:
                    nA[0][1]()
                    for t in range(1, len(nA)):
                        if t + 1 < len(nA):
                            nA[t + 1][0]()
                        nA[t][1]()

            def genC():
                f1s, f2s, f3s = [], [], []
                for t, i in enumerate(tiles):
                    def f1(t=t, i=i):
                        halves = []
                        for hh in range(2):
                            bo, Bbo = nbank()
                            halves.append((bo, Bbo))

                            def f(e, bo=bo, hh=hh, t=t):
                                ins = None
                                for ec in range(8):
                                    ins = e.matmul(bo[:, :], lhsT=ycb[:, ec, t * 128:(t + 1) * 128], rhs=w_out[:, ec, hh * 512:(hh + 1) * 512],
                                                   start=(ec == 0), stop=(ec == 7))
                                return ins
                            P.op("pe", f, Byc + [B_wout], [Bbo], name="outproj")
                        for hh in range(2):
                            bo, Bbo = halves[hh]
                            P.op("dve", lambda e, bo=bo, hh=hh, i=i: e.tensor_tensor(
                                out=x1[:, i, hh * 512:(hh + 1) * 512], in0=x1[:, i, hh * 512:(hh + 1) * 512], in1=bo[:, :], op=ALU.add),
                                [Bbo, B_x1[i]], [B_x1[i]], name="resid1")
                    fr, bk_ = rms_to_T(x1[:, i, :], B_x1[i], g2r, B_g2, h2T, (i * 128, (i + 1) * 128), B_h2T[i], defer="both")
                    f1s.append(f1)
                    f2s.append(fr)
                    f3s.append(bk_)
                n_t = len(tiles)
                for step in range(n_t + 3):
                    if 0 <= step - 3 < n_t:
                        f3s[step - 3]()
                    if step < n_t:
                        f1s[step]()
                    if 0 <= step - 1 < n_t:
                        f2s[step - 1]()
                    yield

            return genB(), genC()

        unit_backs = []
        UNIT_LAG = 0
        reload_after_store = {(0, i): 8 + i for i in range(4)}

        def flush_units():
            while unit_backs:
                unit_backs.pop(0)()

        def ffn_unit(part, jj, tiles, is_sample, reload_up):
            nt = len(tiles)
            NT = nt * 128
            c0 = tiles[0] * 128
            s = up_slot[(part, jj)]
            h2bufs = [B_h2T[i] for i in tiles]
            accs = []
            for ab in range(2):
                bk, Bbk = nbank_up()

                def f(e, bk=bk, ab=ab):
                    ins = None
                    for d in range(8):
                        ins = e.matmul(bk[:, 0:NT], lhsT=wup_r[s][:, ab, d, :], rhs=h2T[:, d, c0:c0 + NT], start=(d == 0), stop=(d == 7))
                    return ins
                P.op("pe", f, h2bufs + [B_wup[s][ab]], [Bbk], name=f"up{ab}")
                ch = ab * NPAIR + jj
                sx, Bsx = nscr()
                acc, Bacc = nscr()
                TS = SPC if is_sample else 1
                HF = 2 * TS
                if is_sample:
                    P.op("pool", lambda e, sx=sx, ch=ch: e.tensor_copy(out=sx[:, 0:HF], in_=hfs[:, ch, :]),
                         [B_hfs[jj]], [Bsx], name="hist_f", tiny=True)
                else:
                    P.op("act", lambda e, sx=sx, ab=ab: e.activation(out=sx[:, 0:2], in_=Hf[:, jj, ab, :], func=AF.Copy),
                         [B_Hf[jj]], [Bsx], name="halo_f", tiny=True)
                P.op("act", lambda e, sx=sx, bk=bk: e.activation(out=sx[:, HF:HF + NT], in_=bk[:, 0:NT], func=AF.Copy),
                     [Bbk], [Bsx], name="evac_up")
                P.op("act", lambda e, acc=acc, bk=bk, ch=ch: e.activation(out=acc[:, 0:NT], in_=bk[:, 0:NT], func=AF.Identity,
                                                                         scale=fcw(2, ch)),
                     [Bbk, B_small], [Bacc], name="evac_up_s")
                for k in (1, 0):
                    P.op("dve", lambda e, acc=acc, sx=sx, k=k, ch=ch: e.scalar_tensor_tensor(
                        out=acc[:, 0:NT], in0=sx[:, k * TS:k * TS + NT], scalar=fcw(k, ch), in1=acc[:, 0:NT], op0=ALU.mult, op1=ALU.add),
                        [Bsx, B_small, Bacc], [Bacc], name=f"fcv{k}")
                if is_sample:
                    P.op("pool", lambda e, sx=sx, ch=ch: e.tensor_copy(out=hfs[:, ch, :], in_=sx[:, NT:NT + HF]),
                         [Bsx], [B_hfs[jj]], name="hist_f_save", tiny=True)
                else:
                    P.op("act", lambda e, sx=sx, ab=ab: e.activation(out=Hf[:, jj, ab, :], in_=sx[:, NT:NT + 2], func=AF.Copy),
                         [Bsx], [B_Hf[jj]], name="halo_f_save", tiny=True)
                accs.append((acc, Bacc))
            if reload_up:
                load_up()
            (acc_a, Bacc_a), (acc_b, Bacc_b) = accs
            ka = nxt("act", 2 * GROUP)

            def back():
                sa, Bsa = nscr()
                P.op("act", lambda e, sa=sa, acc_a=acc_a: e.activation(out=sa[:, 0:NT], in_=acc_a[:, 0:NT], func=AF.Silu),
                     [Bacc_a], [Bsa], name="silu")
                P.op("dve", lambda e, sa=sa, acc_b=acc_b, ka=ka: e.tensor_tensor(
                    out=actb[ka][:, 0:NT], in0=sa[:, 0:NT], in1=acc_b[:, 0:NT], op=ALU.mult), [Bsa, Bacc_b], [B_act[ka]], name="gate")
            unit_backs.append(back)
            while len(unit_backs) > UNIT_LAG:
                unit_backs.pop(0)()
            return ka

        def ffn_down(part, group, kas, tiles, last_group, glob_tile0, only=None, wide=False):
            wide_seq = [4, 5, 6, 7, 0, 1, 2, 3]
            wide_n = [0]

            def nbank_flush():
                k = wide_seq[wide_n[0] % 8]
                wide_n[0] += 1
                return banks[k], B_bank[k]
            for t, i in enumerate(tiles):
                if only is not None and t not in only:
                    continue
                halves = []
                for hh in range(2):
                    bo, Bbo = nbank_flush() if wide else nbank_dn()
                    halves.append((bo, Bbo))

                    def f(e, bo=bo, hh=hh, t=t, i=i):
                        ins = None
                        if last_group:
                            ins = e.matmul(bo[:, :], lhsT=identF[:, :], rhs=x1[:, i, hh * 512:(hh + 1) * 512], start=True, stop=False)
                        for q, jj in enumerate(group):
                            s = dn_slot[(part, jj)]
                            ins = e.matmul(bo[:, :], lhsT=actb[kas[q]][:, t * 128:(t + 1) * 128], rhs=wdn_r[s][:, hh * 512:(hh + 1) * 512],
                                           start=(q == 0 and not last_group), stop=(q == len(group) - 1))
                        return ins
                    rds = [B_act[k] for k in kas] + [B_wdn[dn_slot[(part, jj)]] for jj in group]
                    if last_group:
                        rds = rds + [B_x1[i], B_identF]
                    P.op("pe", f, rds, [Bbo], name="down")
                if not last_group:
                    for hh in range(2):
                        bo, Bbo = halves[hh]
                        P.op("dve", lambda e, bo=bo, hh=hh, i=i: e.tensor_tensor(
                            out=x1[:, i, hh * 512:(hh + 1) * 512], in0=x1[:, i, hh * 512:(hh + 1) * 512], in1=bo[:, :], op=ALU.add),
                            [Bbo, B_x1[i]], [B_x1[i]], name="resid2")
                else:
                    s = nxt("st", 16)
                    c0 = s * 3
                    for hh in range(2):
                        bo, Bbo = halves[hh]
                        P.op("act", lambda e, bo=bo, hh=hh, c0=c0: e.activation(out=ycat[:, hh, :], in_=bo[:, :], func=AF.Square,
                                                                               accum_out=stt[:, c0 + hh:c0 + hh + 1]),
                             [Bbo], [B_ycat[0][hh], B_st[s]], name="fsq")
                    P.op("pool", lambda e, c0=c0: e.tensor_tensor(out=stt[:, c0:c0 + 1], in0=stt[:, c0:c0 + 1], in1=stt[:, c0 + 1:c0 + 2],
                                                                  op=ALU.add), [B_st[s]], [B_st[s]], name="fss", tiny=True)
                    P.op("pool", lambda e, c0=c0: e.tensor_scalar(out=stt[:, c0 + 1:c0 + 2], in0=stt[:, c0:c0 + 1], scalar1=1.0 / D,
                                                                  scalar2=EPS, op0=ALU.mult, op1=ALU.add), [B_st[s]], [B_st[s]], name="fms", tiny=True)
                    P.op("pool", lambda e, c0=c0: e.tensor_tensor(out=stt[:, c0 + 2:c0 + 3], in0=stt[:, c0 + 1:c0 + 2], in1=mhalf[:],
                                                                  op=ALU.pow), [B_st[s], B_mhalf], [B_st[s]], name="fpow", tiny=True)
                    gt = glob_tile0 + t
                    src = x1[:, i, :]

                    def fin(i=i, src=src, c0=c0, s=s, gt=gt, halves=halves):
                        for hh in range(2):
                            bo, Bbo = halves[hh]
                            P.op("dve", lambda e, bo=bo, hh=hh, c0=c0, i=i: e.scalar_tensor_tensor(
                                out=x1[:, i, hh * 512:(hh + 1) * 512], in0=bo[:, :], scalar=stt[:, c0 + 2:c0 + 3],
                                in1=gfr[:, hh * 512:(hh + 1) * 512], op0=ALU.mult, op1=ALU.mult),
                                [Bbo, B_st[s], B_gf], [B_x1[i]], name="fnorm")
                        dma("sp", y_d[gt * 128:(gt + 1) * 128, :], src, [B_x1[i]], [B_out], f"st_y{gt}")
                        if (part, i) in reload_after_store:
                            if i >= 1:
                                load_x(i - 1, reload_after_store[(part, i - 1)])
                            if i == 3:
                                load_x(3, reload_after_store[(part, 3)])
                    final_pending.append(fin)
                    while len(final_pending) > (0 if wide else 1):
                        final_pending.pop(0)()

        final_pending = []

        def flush_final():
            while final_pending:
                final_pending.pop(0)()

        gsz = [2, 2] + [GROUP] * ((NPAIR - 4) // GROUP)
        assert sum(gsz) == NPAIR
        groups = []
        for z in gsz:
            groups.append(list(range(sum(len(g_) for g_ in groups), sum(len(g_) for g_ in groups) + z)))
        parts = [
            [([0, 1, 2, 3], False, True, 0), ([4, 5, 6, 7], False, False, 4)],
            [([0, 1, 2, 3], False, False, 8), ([4, 5, 6, 7], False, False, 12), ([8], True, False, 16)],
        ]
        for part, blocks in enumerate(parts):
            if part == 1:
                for i in range(4, 9):
                    load_x(i, 8 + i)
            if part == 0:
                stage_A(blocks[0][0])
            if part == 0:
                load_rest_of_mixer_weights()
                late_loads()
            prevC = None
            for bi, (tiles, is_sample, first_prompt, gt0) in enumerate(blocks):
                nxt_tiles = blocks[bi + 1][0] if bi + 1 < len(blocks) else None
                par = bi % 2

                def finish_prev(pc=prevC):
                    if pc is not None:
                        for _ in pc:
                            pass
                gB, gC = mixer_block(part, tiles, is_sample, first_prompt, nxt_tiles, par, 1 - par, before_nA=finish_prev,
                                     early_A=not (part == 0 and bi == 0))
                for _ in gB:
                    if prevC is not None:
                        next(prevC, None)
                finish_prev()
                prevC = gC
                if part == 0 and bi == 0:
                    for _ in range(RING_UP):
                        load_up(extra_reads=[B_wout])
            for _ in prevC:
                pass
            for _ in range(RING_DN):
                load_dn(only_part=part)
            if part == 1:
                dma("sp", npp_d.rearrange("p (c k) -> p c k", c=4), Hu[:], B_Hu, [B_out], "st_npp")
                dma("sp", ncp_d.rearrange("p (c k) -> p c k", c=4), Hv[:], B_Hv, [B_out], "st_ncp")
                dma("sp", nps_d.rearrange("p (c q) -> p c q", c=4), ues[:, :, 8 * SPC:23 * SPC], B_ues, [B_out], "st_nps")
                dma("sp", ncs_d.rearrange("p (c q) -> p c q", c=4), ves[:, :, 8 * SPC:10 * SPC], B_ves, [B_out], "st_ncs")
            fblocks = [b_ for b_ in blocks if b_[1]] + [b_ for b_ in blocks if not b_[1]]
            prev = None
            for gi, group in enumerate(groups):
                last_group = gi == len(groups) - 1
                for bi, (tiles, is_sample, first_prompt, gt0) in enumerate(fblocks):
                    last_blk = bi == len(fblocks) - 1
                    kas = []
                    todo = list(range(len(prev[0][2]))) if prev is not None else []
                    nun = len(group)
                    for q, jj in enumerate(group):
                        kas.append(ffn_unit(part, jj, tiles, is_sample, last_blk))
                        if prev is not None:
                            n_now = (len(prev[0][2]) * (q + 1)) // nun - (len(prev[0][2]) * q) // nun
                            if q == nun - 1:
                                n_now = len(todo)
                            if last_group and part + 1 < len(parts) and last_blk:
                                nT_ = len(prev[0][2])
                                n_now = min(len(todo), -((-nT_ * (q + 1)) // nun) + ((-nT_ * q) // nun))
                                if q == nun - 1:
                                    n_now = len(todo)
                            sel = todo[:n_now]
                            todo = todo[n_now:]
                            if sel:
                                ffn_down(part, *prev[0], only=sel)
                                if last_group and part + 1 < len(parts) and last_blk and not todo:
                                    flush_final()
                    if prev is not None and prev[1]:
                        for _ in prev[0][0]:
                            load_dn(only_part=part)
                    prev = ((group, kas, tiles, last_group, gt0), last_blk)
            flush_units()
            nxtA = None
            if part + 1 < len(parts):
                nb = parts[part + 1][0]
                xhb = {2: (hbB, B_hbB), 3: (hbC, B_hbC)}
                nxtA = [rms_to_T(x1[:, i, :], B_x1[i], g1r, B_g1, HT[0], (t * 128, (t + 1) * 128), B_hT[0][t], defer="both",
                                 fixed_hb=xhb.get(t)) for t, i in enumerate(nb[0])]
            if nxtA is not None:
                flush_final()
                nxtA[0][0]()
                nxtA[1][0]()
            ffn_down(part, *prev[0], wide=True)
            flush_final()
            for _ in prev[0][0]:
                load_dn(only_part=part)
            if nxtA is not None:
                for t in range(2, len(nxtA)):
                    nxtA[t][0]()
                cnt["bank"] = 4
                for t in range(len(nxtA)):
                    nxtA[t][1]()
        dma("sp", nfp_d.rearrange("p (j a k) -> p j a k", j=NPAIR, a=2), Hf[:], B_Hf, [B_out], "st_nfp")
        dma("sp", nfs_d.rearrange("p (c q) -> p c q", c=44), hfs[:], B_hfs, [B_out], "st_nfs")

        DMA_K = {"sp": 8, "pool": 32, "act": 1, "pe": 1, "dve": 1}
        streams = P.schedule(DMA_K)
        sem_eng = {e: es.enter_context(nc.semaphore(f"s_{e}")) for e in ("pe", "act", "dve", "pool")}
        sem_dma = {e: [es.enter_context(nc.semaphore(f"d_{e}{i}")) for i in range(DMA_K[e])] for e in ("sp", "pool")}
        block = es.enter_context(nc.Block())

        def emit(e, name):
            for o in streams[name]:
                for d in o.waits:
                    if d.is_dma:
                        e.wait_ge(sem_dma[d.eng][d.dma_sem], d.dma_val)
                    else:
                        e.wait_ge(sem_eng[d.eng], d.sigidx)
                ins = o.fn(e)
                if o.is_dma:
                    ins.then_inc(sem_dma[name][o.dma_sem], 16)
                elif o.signal:
                    ins.then_inc(sem_eng[name], 1)
            h = P.dma_hist[name]
            for o in h[-DMA_K[name]:]:
                e.wait_ge(sem_dma[name][o.dma_sem], o.dma_val)

        @block.sync
        def _(e):
            emit(e, "sp")

        @block.gpsimd
        def _(e):
            emit(e, "pool")

        @block.scalar
        def _(e):
            emit(e, "act")

        @block.vector
        def _(e):
            emit(e, "dve")

        @block.tensor
        def _(e):
            emit(e, "pe")
    return nc


_CACHE = {}


def _small_params(pool_scale, conv_w, ffn_conv_w):
    sm = np.zeros((128, 208), np.float32)
    sm[:, 0:4] = pool_scale.reshape(4, 128).T
    for k in range(3):
        sm[:, 4 + k * 4:4 + (k + 1) * 4] = conv_w[k].reshape(4, 128).T
        sm[:, 16 + k * 44:16 + (k + 1) * 44] = ffn_conv_w[k].reshape(44, 128).T
    for c, win in enumerate(WINS):
        for t in range(15):
            sm[:, 148 + c * 15 + t] = np.float32(1.0) / np.float32(min(t + 1, win))
    return sm


def _fm(a, nchunk):
    S, K, C = a.shape
    t = a.reshape(S, K, nchunk, 128).transpose(3, 2, 1, 0)
    return np.ascontiguousarray(t).reshape(128, nchunk * S * K)


def _unfm(a, nchunk, S, K):
    t = a.reshape(128, nchunk, K, S).transpose(3, 2, 1, 0)
    return np.ascontiguousarray(t).reshape(S, K, nchunk * 128)


def kernel(x_prompt, x_sample, state_pool, state_conv, state_ffn, norm1_g, w_in, w_pool_grp, pool_scale, conv_w,
           w_out, norm2_g, w_up, ffn_conv_w, w_down, final_g):
    f = lambda a: np.ascontiguousarray(np.asarray(a, dtype=np.float32))
    x_prompt, x_sample = f(x_prompt), f(x_sample)
    state_pool, state_conv, state_ffn = f(state_pool), f(state_conv), f(state_ffn)
    if "nc" not in _CACHE:
        _CACHE["nc"] = build_program()
    nc = _CACHE["nc"]
    small = _small_params(f(pool_scale)[0], f(conv_w)[0], f(ffn_conv_w)[0])
    shared = {
        "g1": f(norm1_g)[0], "g2": f(norm2_g)[0], "gf": f(final_g),
        "w_in": f(w_in)[0], "w_pool": f(w_pool_grp)[0], "w_out": f(w_out)[0],
        "w_up": f(w_up)[0], "w_down": f(w_down)[0], "small": small,
        "ident": np.eye(128, dtype=np.float32),
    }
    in_maps = []
    for c in range(NCORES):
        xs = x_sample[c * SPC:(c + 1) * SPC].transpose(1, 0, 2).reshape(SPC * DEC_T, D)
        m = dict(shared)
        m["x"] = np.ascontiguousarray(np.concatenate([x_prompt[c], xs], axis=0))
        m["spT"] = _fm(state_pool[0, c * SPC:(c + 1) * SPC], 4)
        m["scT"] = _fm(state_conv[0, c * SPC:(c + 1) * SPC], 4)
        m["sfT"] = _fm(state_ffn[0, c * SPC:(c + 1) * SPC], 44)
        in_maps.append(m)
    res = run_bass_kernel_spmd(nc, in_maps, core_ids=list(range(NCORES)))
    R = res.results
    y_prompt = np.stack([R[c]["y"][:SEQ] for c in range(NCORES)], axis=0)
    y_sample = np.concatenate([R[c]["y"][SEQ:].reshape(DEC_T, SPC, D).transpose(1, 0, 2) for c in range(NCORES)], axis=0)
    npp = np.stack([_unfm(R[c]["npp"], 4, 1, 15)[0] for c in range(NCORES)], axis=0)[None]
    ncp = np.stack([_unfm(R[c]["ncp"], 4, 1, 2)[0] for c in range(NCORES)], axis=0)[None]
    nfp_l = []
    for c in range(NCORES):
        a = R[c]["nfp"].reshape(128, NPAIR, 2, 2).transpose(0, 2, 1, 3)
        a = np.ascontiguousarray(a).reshape(128, 44 * 1 * 2)
        nfp_l.append(_unfm(a, 44, 1, 2)[0])
    nfp = np.stack(nfp_l, axis=0)[None]
    nps = np.concatenate([_unfm(R[c]["nps"], 4, SPC, 15) for c in range(NCORES)], axis=0)[None]
    ncs = np.concatenate([_unfm(R[c]["ncs"], 4, SPC, 2) for c in range(NCORES)], axis=0)[None]
    nfs = np.concatenate([_unfm(R[c]["nfs"], 44, SPC, 2) for c in range(NCORES)], axis=0)[None]
    return (y_prompt, y_sample, npp, ncp, nfp, nps, ncs, nfs)
```

```python
import numpy as np
from contextlib import ExitStack

import concourse.bass as bass
import concourse.mybir as mybir
from concourse.bass_utils import run_bass_kernel_spmd

F32 = mybir.dt.float32
BF16 = mybir.dt.bfloat16
AF = mybir.ActivationFunctionType
ALU = mybir.AluOpType

D = 1024
SEQ = 2048
NCORES = 8
DEC_B = 128
DEC_T = 8
SPC = DEC_B // NCORES
PW = 512
DFF = 2816
NPAIR = DFF // 128
WINS = (2, 4, 8, 16)
EPS = 1e-6
NTILE = 17
GROUP = 3
RING_UP = GROUP + 1
RING_DN = 2 * GROUP
NSCR = 10
SCRW = 528

ENGS = ("pe", "act", "dve", "pool", "sp")
STRICT_SAME_ENGINE = False


class Buf:
    def __init__(self, name):
        self.name = name
        self.last_w = None
        self.readers = []
        self.aliases = []


class Op:
    __slots__ = ("eng", "fn", "reads", "writes", "is_dma", "key", "idx", "pos", "waits",
                 "signal", "sigidx", "clock", "dma_sem", "dma_val", "name", "tiny")

    def __init__(self, eng, fn, reads, writes, is_dma, key, idx, name):
        self.eng = eng
        self.fn = fn
        self.reads = reads
        self.writes = writes
        self.is_dma = is_dma
        self.key = key
        self.idx = idx
        self.name = name
        self.waits = []
        self.signal = False
        self.sigidx = None
        self.clock = None
        self.dma_sem = None
        self.dma_val = None
        self.pos = None


class Prog:
    def __init__(self):
        self.ops = []
        self.t = 0.0

    def op(self, eng, fn, reads=(), writes=(), dma=False, lag=0.0, name="", tiny=False):
        o = Op(eng, fn, list(reads), list(writes), dma, self.t + lag, len(self.ops), name)
        o.tiny = tiny
        self.ops.append(o)
        return o

    def schedule(self, dma_k):
        ops = sorted(self.ops, key=lambda o: (o.key, o.idx))
        streams = {e: [] for e in ENGS}
        seen = {e: {f: -1 for f in ENGS} for e in ENGS}
        known_dma = {e: set() for e in ENGS}
        dma_hist = {e: [] for e in ENGS}
        for o in ops:
            o.pos = len(streams[o.eng])
            streams[o.eng].append(o)
            deps = []
            for b0 in o.reads:
                for b in [b0] + b0.aliases:
                    if b.last_w is not None:
                        deps.append(b.last_w)
            for b0 in o.writes:
                for b in [b0] + b0.aliases:
                    if b.last_w is not None:
                        deps.append(b.last_w)
                    deps.extend(b.readers)
            rd = o.reads
            wr = o.writes
            if o.is_dma:
                h = dma_hist[o.eng]
                kq = dma_k[o.eng]
                if len(h) >= kq:
                    deps.append(h[len(h) - kq])
                n = len(h)
                o.dma_sem = n % kq
                o.dma_val = 16 * (n // kq + 1)
                h.append(o)
            for b in rd:
                b.readers.append(o)
            for b in wr:
                b.last_w = o
                b.readers = []
            best = {}
            for d in deps:
                if d is o:
                    continue
                if d.is_dma:
                    if d not in known_dma[o.eng]:
                        known_dma[o.eng].add(d)
                        o.waits.append(d)
                else:
                    if d.eng == o.eng and not o.is_dma:
                        if (STRICT_SAME_ENGINE or (d.tiny and o.pos - d.pos <= 8)) and seen[o.eng][o.eng] < d.pos:
                            if d.eng not in best or best[d.eng].pos < d.pos:
                                best[d.eng] = d
                        continue
                    if seen[o.eng][d.eng] >= d.pos:
                        continue
                    if d.eng not in best or best[d.eng].pos < d.pos:
                        best[d.eng] = d
            for f, d in best.items():
                if seen[o.eng][f] >= d.pos:
                    continue
                d.signal = True
                o.waits.append(d)
                for g2, p in d.clock.items():
                    if seen[o.eng][g2] < p:
                        seen[o.eng][g2] = p
            if not o.is_dma:
                c = dict(seen[o.eng])
                c[o.eng] = o.pos
                o.clock = c
        for e in ENGS:
            n = 0
            for o in streams[e]:
                if o.signal:
                    n += 1
                    o.sigidx = n
        self.streams = streams
        self.dma_hist = dma_hist
        return streams


def build_program():
    nc = bass.Bass("TRN2", target_bir_lowering=False)
    P = Prog()

    def dram(name, shape, kind):
        return nc.dram_tensor(name, list(shape), F32, kind=kind).ap()

    x_d = dram("x", [NTILE * 128, D], "ExternalInput")
    spT_d = dram("spT", [128, 4 * SPC * 15], "ExternalInput")
    scT_d = dram("scT", [128, 4 * SPC * 2], "ExternalInput")
    sfT_d = dram("sfT", [128, 44 * SPC * 2], "ExternalInput")
    g1_d = dram("g1", [D], "ExternalInput")
    g2_d = dram("g2", [D], "ExternalInput")
    gf_d = dram("gf", [D], "ExternalInput")
    win_d = dram("w_in", [D, 2048], "ExternalInput")
    wpool_d = dram("w_pool", [4, 128, 128], "ExternalInput")
    wout_d = dram("w_out", [D, D], "ExternalInput")
    wup_d = dram("w_up", [D, 2 * DFF], "ExternalInput")
    wdn_d = dram("w_down", [DFF, D], "ExternalInput")
    small_d = dram("small", [128, 4 + 12 + 132 + 60], "ExternalInput")
    ident_d = dram("ident", [128, 128], "ExternalInput")

    y_d = dram("y", [NTILE * 128, D], "ExternalOutput")
    npp_d = dram("npp", [128, 4 * 15], "ExternalOutput")
    ncp_d = dram("ncp", [128, 4 * 2], "ExternalOutput")
    nfp_d = dram("nfp", [128, NPAIR * 2 * 2], "ExternalOutput")
    nps_d = dram("nps", [128, 4 * SPC * 15], "ExternalOutput")
    ncs_d = dram("ncs", [128, 4 * SPC * 2], "ExternalOutput")
    nfs_d = dram("nfs", [128, 44 * SPC * 2], "ExternalOutput")

    es = ExitStack()
    with es:
        def sb(name, shape, dt):
            return es.enter_context(nc.sbuf_tensor("sb_" + name, list(shape), dt))

        x1 = sb("x1", [128, 9, D], F32)
        h2T = sb("h2T", [128, 8, 1152], BF16)
        g1r = sb("g1r", [128, D], F32)
        g2r = sb("g2r", [128, D], F32)
        gfr = sb("gfr", [128, D], F32)
        small = sb("small", [128, 208], F32)
        mhalf = sb("mhalf", [128, 1], F32)
        epsb = sb("epsb", [128, 1], F32)
        stt = sb("stt", [128, 48], F32)
        w_in = sb("w_in_s", [128, 8, 2048], BF16)
        w_out = sb("w_out_s", [128, 8, D], BF16)
        w_pool = sb("w_pool_s", [128, 4, 128], BF16)
        ident = sb("ident", [128, 128], BF16)
        identF = sb("identF", [128, 128], F32)
        hb = [sb(f"hb{i}", [128, D], BF16) for i in range(2)]
        hT = sb("hT", [128, 8, 512], BF16)
        ycat = sb("ycat", [128, 8, 512], BF16)
        scr = [sb(f"scr{i}", [128, SCRW], F32) for i in range(NSCR)]
        dT = [sb(f"dT{i}", [128, 512], BF16) for i in range(2)]
        Hu = sb("Hu", [128, 4, 15], F32)
        Hv = sb("Hv", [128, 4, 2], F32)
        Hf = sb("Hf", [128, NPAIR, 2, 2], F32)
        ues = sb("ues", [128, 4, 23 * SPC], F32)
        ves = sb("ves", [128, 4, 10 * SPC], F32)
        hfs = sb("hfs", [128, 44, 2 * SPC], F32)
        wup_r = [sb(f"wup{i}", [128, 2, 8, 128], BF16) for i in range(RING_UP)]
        assert RING_DN == 6 and GROUP == 3
        arena = sb("arena", [128, RING_DN * D + 2 * GROUP * 512], BF16)
        wdn_r = [arena[:, i * D:(i + 1) * D] for i in range(RING_DN)]
        actb = [arena[:, RING_DN * D + i * 512:RING_DN * D + (i + 1) * 512] for i in range(2 * GROUP)]
        hT2 = arena[:, 0:4096].rearrange("p (c t) -> p c t", c=8)
        ycat2 = arena[:, 4096:8192].rearrange("p (c t) -> p c t", c=8)
        HT = [hT, hT2]
        YC = [ycat, ycat2]
        hbA = arena[:, 8192:9216]
        banks = [es.enter_context(nc.psum_tensor(f"bank{i}", [128, 512], F32)) for i in range(8)]

        pscale = small[:, 0:4]

        def cw(k, c):
            return small[:, 4 + k * 4 + c: 4 + k * 4 + c + 1]

        def fcw(k, ch):
            return small[:, 16 + k * 44 + ch: 16 + k * 44 + ch + 1]

        def invc(c):
            return small[:, 148 + c * 15: 148 + (c + 1) * 15]

        B_x1 = [Buf(f"x1_{i}") for i in range(9)]
        B_h2T = [Buf(f"h2T_{i}") for i in range(9)]
        B_g1, B_g2, B_gf, B_small, B_mhalf = Buf("g1"), Buf("g2"), Buf("gf"), Buf("small"), Buf("mhalf")
        B_st = [Buf(f"st{i}") for i in range(16)]
        B_wout, B_wpool, B_ident = Buf("wout"), Buf("wpool"), Buf("ident")
        B_identF = Buf("identF")
        B_winq = [Buf(f"win{q}") for q in range(4)]
        B_hb = [Buf("hb0"), Buf("hb1")]
        B_hT = [[Buf(f"hT{p_}_{i}") for i in range(4)] for p_ in range(2)]
        B_ycat = [[Buf(f"ycat{p_}_{i}") for i in range(8)] for p_ in range(2)]
        B_scr = [Buf(f"scr{i}") for i in range(NSCR)]
        B_dT = [Buf("dT0"), Buf("dT1")]
        B_Hu = [Buf(f"Hu{i}") for i in range(4)]
        B_Hv = [Buf(f"Hv{i}") for i in range(4)]
        B_Hf = [Buf(f"Hf{i}") for i in range(NPAIR)]
        B_ues = [Buf(f"ues{i}") for i in range(4)]
        B_ves = [Buf(f"ves{i}") for i in range(4)]
        B_hfs = [Buf(f"hfs{i}") for i in range(NPAIR)]
        B_wup = [[Buf(f"wup{i}a"), Buf(f"wup{i}b")] for i in range(RING_UP)]
        B_wdn = [Buf(f"wdn{i}") for i in range(RING_DN)]
        B_act = [Buf(f"act{i}") for i in range(2 * GROUP)]

        def alias(a_, b_):
            a_.aliases.append(b_)
            b_.aliases.append(a_)
        for t_ in range(4):
            for i_ in range(4):
                alias(B_hT[1][t_], B_wdn[i_])
        for e_ in range(8):
            alias(B_ycat[1][e_], B_wdn[4 + e_ // 2] if e_ < 4 else B_act[e_ - 4])
        B_hbA = Buf("hbA")
        alias(B_hbA, B_act[4])
        alias(B_hbA, B_act[5])
        hbB = ycat[:, 2:4, :].rearrange("p c t -> p (c t)")
        hbC = ycat[:, 4:6, :].rearrange("p c t -> p (c t)")
        B_hbB, B_hbC = Buf("hbB"), Buf("hbC")
        alias(B_hbB, B_ycat[0][2])
        alias(B_hbB, B_ycat[0][3])
        alias(B_hbC, B_ycat[0][4])
        alias(B_hbC, B_ycat[0][5])
        B_bank = [Buf(f"bank{i}") for i in range(8)]
        B_out = Buf("outs")

        cnt = {"bank": 0, "scr": 0, "hb": 0, "st": 0, "dT": 0, "act": 0, "bank_up": 0, "bank_dn": 0}

        def nxt(kind, n):
            v = cnt[kind] % n
            cnt[kind] += 1
            return v

        def nbank():
            k = nxt("bank", 8)
            return banks[k], B_bank[k]

        def nbank_up():
            k = nxt("bank_up", 4)
            return banks[k], B_bank[k]

        def nbank_dn():
            k = 4 + nxt("bank_dn", 4)
            return banks[k], B_bank[k]

        def nscr():
            k = nxt("scr", NSCR)
            return scr[k], B_scr[k]

        late = []

        def dma(eng, out, in_, reads, writes, name="", lag=0.0):
            P.op(eng, lambda e, out=out, in_=in_: e.dma_start(out=out, in_=in_), reads, writes, dma=True, name=name, lag=lag)

        def bcast(v):
            return v.rearrange("(o d) -> o d", o=1).to_broadcast([128, D])

        def load_x(part_tile, glob_tile, lag=0.0):
            dma("sp", x1[:, part_tile, :], x_d[glob_tile * 128:(glob_tile + 1) * 128, :], [], [B_x1[part_tile]],
                f"ld_x{glob_tile}", lag=lag)

        load_x(0, 0)
        dma("sp", g1r[:], bcast(g1_d), [], [B_g1], "ld_g1")
        for i in range(1, 4):
            load_x(i, i)
        dma("sp", small[:], small_d[:, :], [], [B_small], "ld_small")
        dma("sp", g2r[:], bcast(g2_d), [], [B_g2], "ld_g2")
        def late_loads():
            gate = [B_winq[1]]
            for i in range(4, 8):
                dma("sp", x1[:, i, :], x_d[i * 128:(i + 1) * 128, :], gate, [B_x1[i]], f"ld_x{i}")
            dma("sp", gfr[:], bcast(gf_d), gate, [B_gf], "ld_gf")
            dma("sp", identF[:], ident_d[:, :], gate, [B_identF], "ld_identF")
            dma("sp", ues[:, :, 0:15 * SPC], spT_d.rearrange("p (c q) -> p c q", c=4), gate, B_ues, "ld_sp")
            dma("sp", ves[:, :, 0:2 * SPC], scT_d.rearrange("p (c q) -> p c q", c=4), gate, B_ves, "ld_sc")
            dma("sp", hfs[:], sfT_d.rearrange("p (c q) -> p c q", c=44), gate, B_hfs, "ld_sf")

        dma("pool", ident[:], ident_d[:, :], [], [B_ident], "ld_ident")
        def load_win(q):
            gate_ = [B_x1[1]] if q == 0 else []
            dma("pool", w_in[:, :, q * 512:(q + 1) * 512],
                win_d.rearrange("(c p) e -> p c e", p=128)[:, :, q * 512:(q + 1) * 512], gate_, [B_winq[q]], f"ld_win{q}")

        def load_rest_of_mixer_weights():
            for q in (2, 3, 1):
                load_win(q)
            dma("pool", w_pool[:], wpool_d.rearrange("g c d -> c g d"), [], [B_wpool], "ld_wpool")
            dma("pool", w_out[:], wout_d.rearrange("(c p) e -> p c e", p=128), [], [B_wout], "ld_wout")
        load_win(0)

        P.op("pool", lambda e: e.memset(mhalf[:], -0.5), [], [B_mhalf], name="mhalf")
        P.op("pool", lambda e: e.memset(epsb[:], EPS), [], [B_mhalf], name="epsb")
        P.op("act", lambda e: e.activation(out=stt[:, 47:48], in_=epsb[:], func=AF.Sqrt), [B_mhalf], [Buf("warm")], name="warm")
        P.op("pool", lambda e: e.memset(Hu[:], 0.0), [], B_Hu, name="Hu0")
        P.op("pool", lambda e: e.memset(Hv[:], 0.0), [], B_Hv, name="Hv0")
        P.op("pool", lambda e: e.memset(Hf[:], 0.0), [], B_Hf, name="Hf0")

        up_slot = {}
        dn_slot = {}
        all_pairs = [(0, jj) for jj in range(NPAIR)] + [(1, jj) for jj in range(NPAIR)]
        pend_up = list(all_pairs)
        pend_dn = list(all_pairs)
        ring_n = {"up": 0, "dn": 0}

        def load_up(extra_reads=()):
            if not pend_up:
                return
            part, jj = pend_up.pop(0)
            s_ = ring_n["up"] % RING_UP
            ring_n["up"] += 1
            up_slot[(part, jj)] = s_
            for ab in range(2):
                f0 = ab * DFF + jj * 128
                src = wup_d.rearrange("(c p) f -> p c f", p=128)[:, :, f0:f0 + 128]
                dma("pool", wup_r[s_][:, ab, :, :], src, list(extra_reads), [B_wup[s_][ab]], f"ld_wup{part}_{jj}_{ab}")

        def load_dn(extra_reads=(), only_part=None):
            if not pend_dn:
                return
            if only_part is not None and pend_dn[0][0] != only_part:
                return
            part, jj = pend_dn.pop(0)
            s_ = ring_n["dn"] % RING_DN
            ring_n["dn"] += 1
            dn_slot[(part, jj)] = s_
            dma("pool", wdn_r[s_], wdn_d[jj * 128:(jj + 1) * 128, :], list(extra_reads), [B_wdn[s_]], f"ld_wdn{part}_{jj}")

        def rms_to_T(src_ap, B_src, grep, B_g, dstT, dst_cols, B_dst, lag=0.0, defer=False, fixed_hb=None):
            state = {}

            def front():
                if fixed_hb is None:
                    k = nxt("hb", 2)
                    hbk, Bhbk = hb[k], B_hb[k]
                else:
                    hbk, Bhbk = fixed_hb
                s = nxt("st", 16)
                c0 = s * 3
                state["k"] = (hbk, Bhbk)
                P.op("act", lambda e: e.activation(out=hbk[:, :], in_=src_ap, func=AF.Square, accum_out=stt[:, c0:c0 + 1]),
                     [B_src], [Bhbk, B_st[s]], name="sq", lag=lag, tiny=True)
                P.op("act", lambda e: e.activation(out=stt[:, c0 + 1:c0 + 2], in_=stt[:, c0:c0 + 1], func=AF.Sqrt,
                                                   scale=1.0 / D, bias=epsb[:]),
                     [B_st[s], B_mhalf], [B_st[s]], name="ms", lag=lag, tiny=True)
                P.op("dve", lambda e: e.reciprocal(out=stt[:, c0 + 2:c0 + 3], in_=stt[:, c0 + 1:c0 + 2]),
                     [B_st[s]], [B_st[s]], name="pow", lag=lag, tiny=True)
                P.op("dve", lambda e: e.scalar_tensor_tensor(out=hbk[:, :], in0=src_ap, scalar=stt[:, c0 + 2:c0 + 3], in1=grep[:],
                                                             op0=ALU.mult, op1=ALU.mult),
                     [B_src, B_st[s], B_g], [Bhbk], name="hnorm", lag=lag)

            def back():
                hbk, Bhbk = state["k"]
                bk, Bbk = nbank()
                pT = bk.bitcast(BF16)

                def tr(e):
                    ins = None
                    for c in range(8):
                        ins = e.transpose(out=pT[:, c * 128:(c + 1) * 128], in_=hbk[:, c * 128:(c + 1) * 128], identity=ident[:])
                    return ins
                P.op("pe", tr, [Bhbk, B_ident], [Bbk], name="tr", lag=lag)
                P.op("act", lambda e: e.activation(out=dstT[:, :, dst_cols[0]:dst_cols[1]],
                                                   in_=pT[:, :].rearrange("p (c t) -> p c t", c=8), func=AF.Copy),
                     [Bbk], [B_dst], name="trcp", lag=lag)
            if defer == "both":
                return front, back
            front()
            if defer:
                return back
            back()
            return None

        def stage_A(tiles):
            backs = []
            for t, i in enumerate(tiles):
                backs.append(rms_to_T(x1[:, i, :], B_x1[i], g1r, B_g1, HT[0], (t * 128, (t + 1) * 128), B_hT[0][t], defer=True))
                if len(backs) >= 2:
                    backs.pop(0)()
            for bfn in backs:
                bfn()

        def mixer_block(part, tiles, is_sample, first_prompt, next_tiles, par, next_par, before_nA=None, early_A=True):
            nt = len(tiles)
            NT = nt * 128
            hTb, ycb = HT[par], YC[par]
            Byc = B_ycat[par]
            hT_bufs = [B_hT[par][t] for t in range(nt)]

            def inproj(bk, echunk):
                def f(e):
                    ins = None
                    for d in range(8):
                        ins = e.matmul(bk[:, 0:NT], lhsT=w_in[:, d, echunk * 128:(echunk + 1) * 128], rhs=hTb[:, d, 0:NT],
                                       start=(d == 0), stop=(d == 7))
                    return ins
                return f

            TS = SPC if is_sample else 1
            HU = 15 * TS
            HV = 2 * TS

            def bu_front(c):
                win = WINS[c]
                bk, Bbk = nbank()
                P.op("pe", inproj(bk, c), hT_bufs + [B_winq[0]], [Bbk], name=f"inproj_u{c}")
                L = HU + NT
                if is_sample:
                    ue_flat = ues[:, c, :]
                    Bue = B_ues[c]
                else:
                    ue_flat, Bue = nscr()
                    P.op("dve", lambda e, su=ue_flat, c=c: e.tensor_copy(out=su[:, 0:15], in_=Hu[:, c, :]), [B_Hu[c]], [Bue],
                         name="halo_u", tiny=True)
                P.op("act", lambda e, bk=bk, su=ue_flat: e.activation(out=su[:, HU:L], in_=bk[:, 0:NT], func=AF.Copy),
                     [Bbk], [Bue], name="evac_u")
                tA, BtA = nscr()
                tB, BtB = nscr()
                cur, Bcur = ue_flat, Bue
                tmp = [(tA, BtA), (tB, BtB)]
                w = 1
                step = 0
                while w < win:
                    lo = (15 - (win - 2 * w)) * TS
                    sh = w * TS
                    dst, Bdst = tmp[step % 2]
                    P.op("dve", lambda e, dst=dst, cur=cur, lo=lo, sh=sh, L=L: e.tensor_tensor(
                        out=dst[:, lo:L], in0=cur[:, lo:L], in1=cur[:, lo - sh:L - sh], op=ALU.add),
                        [Bcur], [Bdst], name=f"pool_s{2 * w}")
                    cur, Bcur = dst, Bdst
                    w *= 2
                    step += 1
                kd = nxt("dT", 2)
                P.op("dve", lambda e, cur=cur, ue_flat=ue_flat, kd=kd, win=win: e.scalar_tensor_tensor(
                    out=dT[kd][:, 0:NT], in0=cur[:, HU:L], scalar=1.0 / win, in1=ue_flat[:, HU:L],
                    op0=ALU.mult, op1=ALU.subtract), [Bcur, Bue], [B_dT[kd]], name="pool_d")
                if first_prompt:
                    other, Bother = tmp[step % 2]
                    P.op("pool", lambda e, other=other, cur=cur, c=c: e.tensor_tensor(
                        out=other[:, 0:15], in0=cur[:, 15:30], in1=invc(c), op=ALU.mult),
                        [Bcur, B_small], [Bother], name="pool_fix1", tiny=True)
                    P.op("dve", lambda e, other=other, ue_flat=ue_flat, kd=kd: e.tensor_tensor(
                        out=dT[kd][:, 0:15], in0=other[:, 0:15], in1=ue_flat[:, 15:30], op=ALU.subtract),
                        [Bother, Bue], [B_dT[kd]], name="pool_fix2", tiny=True)
                if not is_sample:
                    P.op("dve", lambda e, ue_flat=ue_flat, c=c: e.tensor_copy(out=Hu[:, c, :], in_=ue_flat[:, NT:NT + 15]),
                         [Bue], [B_Hu[c]], name="halo_u_save", tiny=True)
                return kd

            def bu_back(c, kd):
                bk2, Bbk2 = nbank()
                P.op("pe", lambda e, bk2=bk2, kd=kd, c=c: e.matmul(bk2[:, 0:NT], lhsT=w_pool[:, c, :], rhs=dT[kd][:, 0:NT],
                                                                   start=True, stop=True),
                     [B_dT[kd], B_wpool], [Bbk2], name="poolmm")
                P.op("act", lambda e, bk2=bk2, c=c: e.activation(out=ycb[:, c, 0:NT], in_=bk2[:, 0:NT], func=AF.Identity,
                                                                 scale=pscale[:, c:c + 1]),
                     [Bbk2, B_small], [Byc[c]], name="evac_ypool")

            def bc(c):
                bgc, Bgc = nbank()
                bhv, Bhv = nbank()
                bgb, Bgb = nbank()
                P.op("pe", inproj(bgc, 8 + c), hT_bufs + [B_winq[2]], [Bgc], name=f"inproj_gc{c}")
                P.op("pe", inproj(bhv, 12 + c), hT_bufs + [B_winq[3]], [Bhv], name=f"inproj_hv{c}")
                P.op("pe", inproj(bgb, 4 + c), hT_bufs + [B_winq[1]], [Bgb], name=f"inproj_gb{c}")
                sg, Bsg = nscr()
                P.op("act", lambda e, sg=sg, bgc=bgc: e.activation(out=sg[:, 0:NT], in_=bgc[:, 0:NT], func=AF.Copy),
                     [Bgc], [Bsg], name="evac_gc")
                acc, Bacc = nscr()
                if is_sample:
                    sv = ves[:, c, :]
                    Bsv = B_ves[c]
                else:
                    sv, Bsv = nscr()
                    P.op("dve", lambda e, sv=sv, c=c: e.tensor_copy(out=sv[:, 0:2], in_=Hv[:, c, :]), [B_Hv[c]], [Bsv],
                         name="halo_v", tiny=True)
                P.op("dve", lambda e, sv=sv, bhv=bhv, sg=sg: e.tensor_tensor(
                    out=sv[:, HV:HV + NT], in0=bhv[:, 0:NT], in1=sg[:, 0:NT], op=ALU.mult), [Bhv, Bsg], [Bsv], name="v")
                P.op("dve", lambda e, sv=sv, acc=acc, c=c: e.tensor_scalar(
                    out=acc[:, 0:NT], in0=sv[:, HV:HV + NT], scalar1=cw(2, c), scalar2=None, op0=ALU.mult),
                    [Bsv, B_small], [Bacc], name="cv2")
                for k in (1, 0):
                    P.op("dve", lambda e, sv=sv, acc=acc, k=k, c=c: e.scalar_tensor_tensor(
                        out=acc[:, 0:NT], in0=sv[:, k * TS:k * TS + NT], scalar=cw(k, c), in1=acc[:, 0:NT], op0=ALU.mult, op1=ALU.add),
                        [Bsv, B_small, Bacc], [Bacc], name=f"cv{k}")
                if not is_sample:
                    P.op("dve", lambda e, sv=sv, c=c: e.tensor_copy(out=Hv[:, c, :], in_=sv[:, NT:NT + 2]),
                         [Bsv], [B_Hv[c]], name="halo_v_save", tiny=True)
                P.op("dve", lambda e, bgb=bgb, acc=acc, c=c: e.tensor_tensor(
                    out=ycb[:, 4 + c, 0:NT], in0=bgb[:, 0:NT], in1=acc[:, 0:NT], op=ALU.mult),
                    [Bgb, Bacc], [Byc[4 + c]], name="yconv")

            def genB():
                nA = []
                if next_tiles is not None:
                    nA = [rms_to_T(x1[:, i, :], B_x1[i], g1r, B_g1, HT[next_par], (t * 128, (t + 1) * 128), B_hT[next_par][t],
                                   defer="both", fixed_hb=(hbA, B_hbA) if early_A else None) for t, i in enumerate(next_tiles)]
                asteps = []
                if early_A:
                    for t in range(len(nA)):
                        asteps.append(nA[t][0])
                        asteps.append(nA[t][1])

                def a_step(n=1):
                    for _ in range(n):
                        if asteps:
                            asteps.pop(0)()
                k0 = bu_front(0)
                yield
                k1 = bu_front(1)
                a_step()
                yield
                bc(0)
                yield
                bu_back(0, k0)
                bu_back(1, k1)
                a_step()
                yield
                k2 = bu_front(2)
                a_step()
                yield
                k3 = bu_front(3)
                yield
                bc(1)
                a_step()
                yield
                bu_back(2, k2)
                bu_back(3, k3)
                a_step()
                yield
                if not early_A and nA:
                    for t in range(min(2, len(nA))):
                        nA[t][0]()
                bc(2)
                a_step()
                yield
                a_step()
                bc(3)
                a_step()
                yield
                a_step(8)
                if not early_A and nA:
                    nA[0][1]()
                    for t in range(1, len(nA)):
                        if t + 1 < len(nA):
                            nA[t + 1][0]()
                        nA[t][1]()

            def genC():
                f1s, f2s, f3s = [], [], []
                for t, i in enumerate(tiles):
                    def f1(t=t, i=i):
                        halves = []
                        for hh in range(2):
                            bo, Bbo = nbank()
                            halves.append((bo, Bbo))

                            def f(e, bo=bo, hh=hh, t=t):
                                ins = None
                                for ec in range(8):
                                    ins = e.matmul(bo[:, :], lhsT=ycb[:, ec, t * 128:(t + 1) * 128], rhs=w_out[:, ec, hh * 512:(hh + 1) * 512],
                                                   start=(ec == 0), stop=(ec == 7))
                                return ins
                            P.op("pe", f, Byc + [B_wout], [Bbo], name="outproj")
                        for hh in range(2):
                            bo, Bbo = halves[hh]
                            P.op("dve", lambda e, bo=bo, hh=hh, i=i: e.tensor_tensor(
                                out=x1[:, i, hh * 512:(hh + 1) * 512], in0=x1[:, i, hh * 512:(hh + 1) * 512], in1=bo[:, :], op=ALU.add),
                                [Bbo, B_x1[i]], [B_x1[i]], name="resid1")
                    fr, bk_ = rms_to_T(x1[:, i, :], B_x1[i], g2r, B_g2, h2T, (i * 128, (i + 1) * 128), B_h2T[i], defer="both")
                    f1s.append(f1)
                    f2s.append(fr)
                    f3s.append(bk_)
                n_t = len(tiles)
                for step in range(n_t + 3):
                    if 0 <= step - 3 < n_t:
                        f3s[step - 3]()
                    if step < n_t:
                        f1s[step]()
                    if 0 <= step - 1 < n_t:
                        f2s[step - 1]()
                    yield

            return genB(), genC()

        unit_backs = []
        UNIT_LAG = 0
        reload_after_store = {(0, i): 8 + i for i in range(4)}

        def flush_units():
            while unit_backs:
                unit_backs.pop(0)()

        def ffn_unit(part, jj, tiles, is_sample, reload_up):
            nt = len(tiles)
            NT = nt * 128
            c0 = tiles[0] * 128
            s = up_slot[(part, jj)]
            h2bufs = [B_h2T[i] for i in tiles]
            accs = []
            for ab in range(2):
                bk, Bbk = nbank_up()

                def f(e, bk=bk, ab=ab):
                    ins = None
                    for d in range(8):
                        ins = e.matmul(bk[:, 0:NT], lhsT=wup_r[s][:, ab, d, :], rhs=h2T[:, d, c0:c0 + NT], start=(d == 0), stop=(d == 7))
                    return ins
                P.op("pe", f, h2bufs + [B_wup[s][ab]], [Bbk], name=f"up{ab}")
                ch = ab * NPAIR + jj
                sx, Bsx = nscr()
                acc, Bacc = nscr()
                TS = SPC if is_sample else 1
                HF = 2 * TS
                if is_sample:
                    P.op("pool", lambda e, sx=sx, ch=ch: e.tensor_copy(out=sx[:, 0:HF], in_=hfs[:, ch, :]),
                         [B_hfs[jj]], [Bsx], name="hist_f", tiny=True)
                else:
                    P.op("act", lambda e, sx=sx, ab=ab: e.activation(out=sx[:, 0:2], in_=Hf[:, jj, ab, :], func=AF.Copy),
                         [B_Hf[jj]], [Bsx], name="halo_f", tiny=True)
                P.op("act", lambda e, sx=sx, bk=bk: e.activation(out=sx[:, HF:HF + NT], in_=bk[:, 0:NT], func=AF.Copy),
                     [Bbk], [Bsx], name="evac_up")
                P.op("act", lambda e, acc=acc, bk=bk, ch=ch: e.activation(out=acc[:, 0:NT], in_=bk[:, 0:NT], func=AF.Identity,
                                                                         scale=fcw(2, ch)),
                     [Bbk, B_small], [Bacc], name="evac_up_s")
                for k in (1, 0):
                    P.op("dve", lambda e, acc=acc, sx=sx, k=k, ch=ch: e.scalar_tensor_tensor(
                        out=acc[:, 0:NT], in0=sx[:, k * TS:k * TS + NT], scalar=fcw(k, ch), in1=acc[:, 0:NT], op0=ALU.mult, op1=ALU.add),
                        [Bsx, B_small, Bacc], [Bacc], name=f"fcv{k}")
                if is_sample:
                    P.op("pool", lambda e, sx=sx, ch=ch: e.tensor_copy(out=hfs[:, ch, :], in_=sx[:, NT:NT + HF]),
                         [Bsx], [B_hfs[jj]], name="hist_f_save", tiny=True)
                else:
                    P.op("act", lambda e, sx=sx, ab=ab: e.activation(out=Hf[:, jj, ab, :], in_=sx[:, NT:NT + 2], func=AF.Copy),
                         [Bsx], [B_Hf[jj]], name="halo_f_save", tiny=True)
                accs.append((acc, Bacc))
            if reload_up:
                load_up()
            (acc_a, Bacc_a), (acc_b, Bacc_b) = accs
            ka = nxt("act", 2 * GROUP)

            def back():
                sa, Bsa = nscr()
                P.op("act", lambda e, sa=sa, acc_a=acc_a: e.activation(out=sa[:, 0:NT], in_=acc_a[:, 0:NT], func=AF.Silu),
                     [Bacc_a], [Bsa], name="silu")
                P.op("dve", lambda e, sa=sa, acc_b=acc_b, ka=ka: e.tensor_tensor(
                    out=actb[ka][:, 0:NT], in0=sa[:, 0:NT], in1=acc_b[:, 0:NT], op=ALU.mult), [Bsa, Bacc_b], [B_act[ka]], name="gate")
            unit_backs.append(back)
            while len(unit_backs) > UNIT_LAG:
                unit_backs.pop(0)()
            return ka

        def ffn_down(part, group, kas, tiles, last_group, glob_tile0, only=None, wide=False):
            wide_seq = [4, 5, 6, 7, 0, 1, 2, 3]
            wide_n = [0]

            def nbank_flush():
                k = wide_seq[wide_n[0] % 8]
                wide_n[0] += 1
                return banks[k], B_bank[k]
            for t, i in enumerate(tiles):
                if only is not None and t not in only:
                    continue
                halves = []
                for hh in range(2):
                    bo, Bbo = nbank_flush() if wide else nbank_dn()
                    halves.append((bo, Bbo))

                    def f(e, bo=bo, hh=hh, t=t, i=i):
                        ins = None
                        if last_group:
                            ins = e.matmul(bo[:, :], lhsT=identF[:, :], rhs=x1[:, i, hh * 512:(hh + 1) * 512], start=True, stop=False)
                        for q, jj in enumerate(group):
                            s = dn_slot[(part, jj)]
                            ins = e.matmul(bo[:, :], lhsT=actb[kas[q]][:, t * 128:(t + 1) * 128], rhs=wdn_r[s][:, hh * 512:(hh + 1) * 512],
                                           start=(q == 0 and not last_group), stop=(q == len(group) - 1))
                        return ins
                    rds = [B_act[k] for k in kas] + [B_wdn[dn_slot[(part, jj)]] for jj in group]
                    if last_group:
                        rds = rds + [B_x1[i], B_identF]
                    P.op("pe", f, rds, [Bbo], name="down")
                if not last_group:
                    for hh in range(2):
                        bo, Bbo = halves[hh]
                        P.op("dve", lambda e, bo=bo, hh=hh, i=i: e.tensor_tensor(
                            out=x1[:, i, hh * 512:(hh + 1) * 512], in0=x1[:, i, hh * 512:(hh + 1) * 512], in1=bo[:, :], op=ALU.add),
                            [Bbo, B_x1[i]], [B_x1[i]], name="resid2")
                else:
                    s = nxt("st", 16)
                    c0 = s * 3
                    for hh in range(2):
                        bo, Bbo = halves[hh]
                        P.op("act", lambda e, bo=bo, hh=hh, c0=c0: e.activation(out=ycat[:, hh, :], in_=bo[:, :], func=AF.Square,
                                                                               accum_out=stt[:, c0 + hh:c0 + hh + 1]),
                             [Bbo], [B_ycat[0][hh], B_st[s]], name="fsq")
                    P.op("pool", lambda e, c0=c0: e.tensor_tensor(out=stt[:, c0:c0 + 1], in0=stt[:, c0:c0 + 1], in1=stt[:, c0 + 1:c0 + 2],
                                                                  op=ALU.add), [B_st[s]], [B_st[s]], name="fss", tiny=True)
                    P.op("pool", lambda e, c0=c0: e.tensor_scalar(out=stt[:, c0 + 1:c0 + 2], in0=stt[:, c0:c0 + 1], scalar1=1.0 / D,
                                                                  scalar2=EPS, op0=ALU.mult, op1=ALU.add), [B_st[s]], [B_st[s]], name="fms", tiny=True)
                    P.op("pool", lambda e, c0=c0: e.tensor_tensor(out=stt[:, c0 + 2:c0 + 3], in0=stt[:, c0 + 1:c0 + 2], in1=mhalf[:],
                                                                  op=ALU.pow), [B_st[s], B_mhalf], [B_st[s]], name="fpow", tiny=True)
                    gt = glob_tile0 + t
                    src = x1[:, i, :]

                    def fin(i=i, src=src, c0=c0, s=s, gt=gt, halves=halves):
                        for hh in range(2):
                            bo, Bbo = halves[hh]
                            P.op("dve", lambda e, bo=bo, hh=hh, c0=c0, i=i: e.scalar_tensor_tensor(
                                out=x1[:, i, hh * 512:(hh + 1) * 512], in0=bo[:, :], scalar=stt[:, c0 + 2:c0 + 3],
                                in1=gfr[:, hh * 512:(hh + 1) * 512], op0=ALU.mult, op1=ALU.mult),
                                [Bbo, B_st[s], B_gf], [B_x1[i]], name="fnorm")
                        dma("sp", y_d[gt * 128:(gt + 1) * 128, :], src, [B_x1[i]], [B_out], f"st_y{gt}")
                        if (part, i) in reload_after_store:
                            if i >= 1:
                                load_x(i - 1, reload_after_store[(part, i - 1)])
                            if i == 3:
                                load_x(3, reload_after_store[(part, 3)])
                    final_pending.append(fin)
                    while len(final_pending) > (0 if wide else 1):
                        final_pending.pop(0)()

        final_pending = []

        def flush_final():
            while final_pending:
                final_pending.pop(0)()

        gsz = [2, 2] + [GROUP] * ((NPAIR - 4) // GROUP)
        assert sum(gsz) == NPAIR
        groups = []
        for z in gsz:
            groups.append(list(range(sum(len(g_) for g_ in groups), sum(len(g_) for g_ in groups) + z)))
        parts = [
            [([0, 1, 2, 3], False, True, 0), ([4, 5, 6, 7], False, False, 4)],
            [([0, 1, 2, 3], False, False, 8), ([4, 5, 6, 7], False, False, 12), ([8], True, False, 16)],
        ]
        for part, blocks in enumerate(parts):
            if part == 1:
                for i in range(4, 9):
                    load_x(i, 8 + i)
            if part == 0:
                stage_A(blocks[0][0])
            if part == 0:
                load_rest_of_mixer_weights()
                late_loads()
            prevC = None
            for bi, (tiles, is_sample, first_prompt, gt0) in enumerate(blocks):
                nxt_tiles = blocks[bi + 1][0] if bi + 1 < len(blocks) else None
                par = bi % 2

                def finish_prev(pc=prevC):
                    if pc is not None:
                        for _ in pc:
                            pass
                gB, gC = mixer_block(part, tiles, is_sample, first_prompt, nxt_tiles, par, 1 - par, before_nA=finish_prev,
                                     early_A=not (part == 0 and bi == 0))
                for _ in gB:
                    if prevC is not None:
                        next(prevC, None)
                finish_prev()
                prevC = gC
                if part == 0 and bi == 0:
                    for _ in range(RING_UP):
                        load_up(extra_reads=[B_wout])
            for _ in prevC:
                pass
            for _ in range(RING_DN):
                load_dn(only_part=part)
            if part == 1:
                dma("sp", npp_d.rearrange("p (c k) -> p c k", c=4), Hu[:], B_Hu, [B_out], "st_npp")
                dma("sp", ncp_d.rearrange("p (c k) -> p c k", c=4), Hv[:], B_Hv, [B_out], "st_ncp")
                dma("sp", nps_d.rearrange("p (c q) -> p c q", c=4), ues[:, :, 8 * SPC:23 * SPC], B_ues, [B_out], "st_nps")
                dma("sp", ncs_d.rearrange("p (c q) -> p c q", c=4), ves[:, :, 8 * SPC:10 * SPC], B_ves, [B_out], "st_ncs")
            fblocks = [b_ for b_ in blocks if b_[1]] + [b_ for b_ in blocks if not b_[1]]
            prev = None
            for gi, group in enumerate(groups):
                last_group = gi == len(groups) - 1
                for bi, (tiles, is_sample, first_prompt, gt0) in enumerate(fblocks):
                    last_blk = bi == len(fblocks) - 1
                    kas = []
                    todo = list(range(len(prev[0][2]))) if prev is not None else []
                    nun = len(group)
                    for q, jj in enumerate(group):
                        kas.append(ffn_unit(part, jj, tiles, is_sample, last_blk))
                        if prev is not None:
                            n_now = (len(prev[0][2]) * (q + 1)) // nun - (len(prev[0][2]) * q) // nun
                            if q == nun - 1:
                                n_now = len(todo)
                            if last_group and part + 1 < len(parts) and last_blk:
                                nT_ = len(prev[0][2])
                                n_now = min(len(todo), -((-nT_ * (q + 1)) // nun) + ((-nT_ * q) // nun))
                                if q == nun - 1:
                                    n_now = len(todo)
                            sel = todo[:n_now]
                            todo = todo[n_now:]
                            if sel:
                                ffn_down(part, *prev[0], only=sel)
                                if last_group and part + 1 < len(parts) and last_blk and not todo:
                                    flush_final()
                    if prev is not None and prev[1]:
                        for _ in prev[0][0]:
                            load_dn(only_part=part)
                    prev = ((group, kas, tiles, last_group, gt0), last_blk)
            flush_units()
            nxtA = None
            if part + 1 < len(parts):
                nb = parts[part + 1][0]
                xhb = {2: (hbB, B_hbB), 3: (hbC, B_hbC)}
                nxtA = [rms_to_T(x1[:, i, :], B_x1[i], g1r, B_g1, HT[0], (t * 128, (t + 1) * 128), B_hT[0][t], defer="both",
                                 fixed_hb=xhb.get(t)) for t, i in enumerate(nb[0])]
            if nxtA is not None:
                flush_final()
                for t in range(len(nxtA)):
                    nxtA[t][0]()
            ffn_down(part, *prev[0], wide=True)
            flush_final()
            for _ in prev[0][0]:
                load_dn(only_part=part)
            if nxtA is not None:
                cnt["bank"] = 4
                for t in range(len(nxtA)):
                    nxtA[t][1]()
        dma("sp", nfp_d.rearrange("p (j a k) -> p j a k", j=NPAIR, a=2), Hf[:], B_Hf, [B_out], "st_nfp")
        dma("sp", nfs_d.rearrange("p (c q) -> p c q", c=44), hfs[:], B_hfs, [B_out], "st_nfs")

        DMA_K = {"sp": 8, "pool": 32, "act": 1, "pe": 1, "dve": 1}
        streams = P.schedule(DMA_K)
        sem_eng = {e: es.enter_context(nc.semaphore(f"s_{e}")) for e in ("pe", "act", "dve", "pool")}
        sem_dma = {e: [es.enter_context(nc.semaphore(f"d_{e}{i}")) for i in range(DMA_K[e])] for e in ("sp", "pool")}
        block = es.enter_context(nc.Block())

        def emit(e, name):
            for o in streams[name]:
                for d in o.waits:
                    if d.is_dma:
                        e.wait_ge(sem_dma[d.eng][d.dma_sem], d.dma_val)
                    else:
                        e.wait_ge(sem_eng[d.eng], d.sigidx)
                ins = o.fn(e)
                if o.is_dma:
                    ins.then_inc(sem_dma[name][o.dma_sem], 16)
                elif o.signal:
                    ins.then_inc(sem_eng[name], 1)
            h = P.dma_hist[name]
            for o in h[-DMA_K[name]:]:
                e.wait_ge(sem_dma[name][o.dma_sem], o.dma_val)

        @block.sync
        def _(e):
            emit(e, "sp")

        @block.gpsimd
        def _(e):
            emit(e, "pool")

        @block.scalar
        def _(e):
            emit(e, "act")

        @block.vector
        def _(e):
            emit(e, "dve")

        @block.tensor
        def _(e):
            emit(e, "pe")
    return nc


_CACHE = {}


def _small_params(pool_scale, conv_w, ffn_conv_w):
    sm = np.zeros((128, 208), np.float32)
    sm[:, 0:4] = pool_scale.reshape(4, 128).T
    for k in range(3):
        sm[:, 4 + k * 4:4 + (k + 1) * 4] = conv_w[k].reshape(4, 128).T
        sm[:, 16 + k * 44:16 + (k + 1) * 44] = ffn_conv_w[k].reshape(44, 128).T
    for c, win in enumerate(WINS):
        for t in range(15):
            sm[:, 148 + c * 15 + t] = np.float32(1.0) / np.float32(min(t + 1, win))
    return sm


def _fm(a, nchunk):
    S, K, C = a.shape
    t = a.reshape(S, K, nchunk, 128).transpose(3, 2, 1, 0)
    return np.ascontiguousarray(t).reshape(128, nchunk * S * K)


def _unfm(a, nchunk, S, K):
    t = a.reshape(128, nchunk, K, S).transpose(3, 2, 1, 0)
    return np.ascontiguousarray(t).reshape(S, K, nchunk * 128)


def kernel(x_prompt, x_sample, state_pool, state_conv, state_ffn, norm1_g, w_in, w_pool_grp, pool_scale, conv_w,
           w_out, norm2_g, w_up, ffn_conv_w, w_down, final_g):
    f = lambda a: np.ascontiguousarray(np.asarray(a, dtype=np.float32))
    x_prompt, x_sample = f(x_prompt), f(x_sample)
    state_pool, state_conv, state_ffn = f(state_pool), f(state_conv), f(state_ffn)
    if "nc" not in _CACHE:
        _CACHE["nc"] = build_program()
    nc = _CACHE["nc"]
    small = _small_params(f(pool_scale)[0], f(conv_w)[0], f(ffn_conv_w)[0])
    shared = {
        "g1": f(norm1_g)[0], "g2": f(norm2_g)[0], "gf": f(final_g),
        "w_in": f(w_in)[0], "w_pool": f(w_pool_grp)[0], "w_out": f(w_out)[0],
        "w_up": f(w_up)[0], "w_down": f(w_down)[0], "small": small,
        "ident": np.eye(128, dtype=np.float32),
    }
    in_maps = []
    for c in range(NCORES):
        xs = x_sample[c * SPC:(c + 1) * SPC].transpose(1, 0, 2).reshape(SPC * DEC_T, D)
        m = dict(shared)
        m["x"] = np.ascontiguousarray(np.concatenate([x_prompt[c], xs], axis=0))
        m["spT"] = _fm(state_pool[0, c * SPC:(c + 1) * SPC], 4)
        m["scT"] = _fm(state_conv[0, c * SPC:(c + 1) * SPC], 4)
        m["sfT"] = _fm(state_ffn[0, c * SPC:(c + 1) * SPC], 44)
        in_maps.append(m)
    res = run_bass_kernel_spmd(nc, in_maps, core_ids=list(range(NCORES)))
    R = res.results
    y_prompt = np.stack([R[c]["y"][:SEQ] for c in range(NCORES)], axis=0)
    y_sample = np.concatenate([R[c]["y"][SEQ:].reshape(DEC_T, SPC, D).transpose(1, 0, 2) for c in range(NCORES)], axis=0)
    npp = np.stack([_unfm(R[c]["npp"], 4, 1, 15)[0] for c in range(NCORES)], axis=0)[None]
    ncp = np.stack([_unfm(R[c]["ncp"], 4, 1, 2)[0] for c in range(NCORES)], axis=0)[None]
    nfp_l = []
    for c in range(NCORES):
        a = R[c]["nfp"].reshape(128, NPAIR, 2, 2).transpose(0, 2, 1, 3)
        a = np.ascontiguousarray(a).reshape(128, 44 * 1 * 2)
        nfp_l.append(_unfm(a, 44, 1, 2)[0])
    nfp = np.stack(nfp_l, axis=0)[None]
    nps = np.concatenate([_unfm(R[c]["nps"], 4, SPC, 15) for c in range(NCORES)], axis=0)[None]
    ncs = np.concatenate([_unfm(R[c]["ncs"], 4, SPC, 2) for c in range(NCORES)], axis=0)[None]
    nfs = np.concatenate([_unfm(R[c]["nfs"], 44, SPC, 2) for c in range(NCORES)], axis=0)[None]
    return (y_prompt, y_sample, npp, ncp, nfp, nps, ncs, nfs)
```

```python
import numpy as np
from contextlib import ExitStack

import concourse.bass as bass
import concourse.mybir as mybir
from concourse.bass_utils import run_bass_kernel_spmd

F32 = mybir.dt.float32
BF16 = mybir.dt.bfloat16
AF = mybir.ActivationFunctionType
ALU = mybir.AluOpType

D = 1024
SEQ = 2048
NCORES = 8
DEC_B = 128
DEC_T = 8
SPC = DEC_B // NCORES
PW = 512
DFF = 2816
NPAIR = DFF // 128
WINS = (2, 4, 8, 16)
EPS = 1e-6
NTILE = 17
GROUP = 3
RING_UP = GROUP + 1
RING_DN = 2 * GROUP
NSCR = 10
SCRW = 528

ENGS = ("pe", "act", "dve", "pool", "sp")
STRICT_SAME_ENGINE = False


class Buf:
    def __init__(self, name):
        self.name = name
        self.last_w = None
        self.readers = []
        self.aliases = []


class Op:
    __slots__ = ("eng", "fn", "reads", "writes", "is_dma", "key", "idx", "pos", "waits",
                 "signal", "sigidx", "clock", "dma_sem", "dma_val", "name", "tiny")

    def __init__(self, eng, fn, reads, writes, is_dma, key, idx, name):
        self.eng = eng
        self.fn = fn
        self.reads = reads
        self.writes = writes
        self.is_dma = is_dma
        self.key = key
        self.idx = idx
        self.name = name
        self.waits = []
        self.signal = False
        self.sigidx = None
        self.clock = None
        self.dma_sem = None
        self.dma_val = None
        self.pos = None


class Prog:
    def __init__(self):
        self.ops = []
        self.t = 0.0

    def op(self, eng, fn, reads=(), writes=(), dma=False, lag=0.0, name="", tiny=False):
        o = Op(eng, fn, list(reads), list(writes), dma, self.t + lag, len(self.ops), name)
        o.tiny = tiny
        self.ops.append(o)
        return o

    def schedule(self, dma_k):
        ops = sorted(self.ops, key=lambda o: (o.key, o.idx))
        streams = {e: [] for e in ENGS}
        seen = {e: {f: -1 for f in ENGS} for e in ENGS}
        known_dma = {e: set() for e in ENGS}
        dma_hist = {e: [] for e in ENGS}
        for o in ops:
            o.pos = len(streams[o.eng])
            streams[o.eng].append(o)
            deps = []
            for b0 in o.reads:
                for b in [b0] + b0.aliases:
                    if b.last_w is not None:
                        deps.append(b.last_w)
            for b0 in o.writes:
                for b in [b0] + b0.aliases:
                    if b.last_w is not None:
                        deps.append(b.last_w)
                    deps.extend(b.readers)
            rd = o.reads
            wr = o.writes
            if o.is_dma:
                h = dma_hist[o.eng]
                kq = dma_k[o.eng]
                if len(h) >= kq:
                    deps.append(h[len(h) - kq])
                n = len(h)
                o.dma_sem = n % kq
                o.dma_val = 16 * (n // kq + 1)
                h.append(o)
            for b in rd:
                b.readers.append(o)
            for b in wr:
                b.last_w = o
                b.readers = []
            best = {}
            for d in deps:
                if d is o:
                    continue
                if d.is_dma:
                    if d not in known_dma[o.eng]:
                        known_dma[o.eng].add(d)
                        o.waits.append(d)
                else:
                    if d.eng == o.eng and not o.is_dma:
                        if (STRICT_SAME_ENGINE or (d.tiny and o.pos - d.pos <= 8)) and seen[o.eng][o.eng] < d.pos:
                            if d.eng not in best or best[d.eng].pos < d.pos:
                                best[d.eng] = d
                        continue
                    if seen[o.eng][d.eng] >= d.pos:
                        continue
                    if d.eng not in best or best[d.eng].pos < d.pos:
                        best[d.eng] = d
            for f, d in best.items():
                if seen[o.eng][f] >= d.pos:
                    continue
                d.signal = True
                o.waits.append(d)
                for g2, p in d.clock.items():
                    if seen[o.eng][g2] < p:
                        seen[o.eng][g2] = p
            if not o.is_dma:
                c = dict(seen[o.eng])
                c[o.eng] = o.pos
                o.clock = c
        for e in ENGS:
            n = 0
            for o in streams[e]:
                if o.signal:
                    n += 1
                    o.sigidx = n
        self.streams = streams
        self.dma_hist = dma_hist
        return streams


def build_program():
    nc = bass.Bass("TRN2", target_bir_lowering=False)
    P = Prog()

    def dram(name, shape, kind):
        return nc.dram_tensor(name, list(shape), F32, kind=kind).ap()

    x_d = dram("x", [NTILE * 128, D], "ExternalInput")
    spT_d = dram("spT", [128, 4 * SPC * 15], "ExternalInput")
    scT_d = dram("scT", [128, 4 * SPC * 2], "ExternalInput")
    sfT_d = dram("sfT", [128, 44 * SPC * 2], "ExternalInput")
    g1_d = dram("g1", [D], "ExternalInput")
    g2_d = dram("g2", [D], "ExternalInput")
    gf_d = dram("gf", [D], "ExternalInput")
    win_d = dram("w_in", [D, 2048], "ExternalInput")
    wpool_d = dram("w_pool", [4, 128, 128], "ExternalInput")
    wout_d = dram("w_out", [D, D], "ExternalInput")
    wup_d = dram("w_up", [D, 2 * DFF], "ExternalInput")
    wdn_d = dram("w_down", [DFF, D], "ExternalInput")
    small_d = dram("small", [128, 4 + 12 + 132 + 60], "ExternalInput")
    ident_d = dram("ident", [128, 128], "ExternalInput")

    y_d = dram("y", [NTILE * 128, D], "ExternalOutput")
    npp_d = dram("npp", [128, 4 * 15], "ExternalOutput")
    ncp_d = dram("ncp", [128, 4 * 2], "ExternalOutput")
    nfp_d = dram("nfp", [128, NPAIR * 2 * 2], "ExternalOutput")
    nps_d = dram("nps", [128, 4 * SPC * 15], "ExternalOutput")
    ncs_d = dram("ncs", [128, 4 * SPC * 2], "ExternalOutput")
    nfs_d = dram("nfs", [128, 44 * SPC * 2], "ExternalOutput")

    es = ExitStack()
    with es:
        def sb(name, shape, dt):
            return es.enter_context(nc.sbuf_tensor("sb_" + name, list(shape), dt))

        x1 = sb("x1", [128, 9, D], F32)
        h2T = sb("h2T", [128, 8, 1152], BF16)
        g1r = sb("g1r", [128, D], F32)
        g2r = sb("g2r", [128, D], F32)
        gfr = sb("gfr", [128, D], F32)
        small = sb("small", [128, 208], F32)
        mhalf = sb("mhalf", [128, 1], F32)
        epsb = sb("epsb", [128, 1], F32)
        stt = sb("stt", [128, 48], F32)
        w_in = sb("w_in_s", [128, 8, 2048], BF16)
        w_out = sb("w_out_s", [128, 8, D], BF16)
        w_pool = sb("w_pool_s", [128, 4, 128], BF16)
        ident = sb("ident", [128, 128], BF16)
        identF = sb("identF", [128, 128], F32)
        hb = [sb(f"hb{i}", [128, D], BF16) for i in range(2)]
        hT = sb("hT", [128, 8, 512], BF16)
        ycat = sb("ycat", [128, 8, 512], BF16)
        scr = [sb(f"scr{i}", [128, SCRW], F32) for i in range(NSCR)]
        dT = [sb(f"dT{i}", [128, 512], BF16) for i in range(2)]
        Hu = sb("Hu", [128, 4, 15], F32)
        Hv = sb("Hv", [128, 4, 2], F32)
        Hf = sb("Hf", [128, NPAIR, 2, 2], F32)
        ues = sb("ues", [128, 4, 23 * SPC], F32)
        ves = sb("ves", [128, 4, 10 * SPC], F32)
        hfs = sb("hfs", [128, 44, 2 * SPC], F32)
        wup_r = [sb(f"wup{i}", [128, 2, 8, 128], BF16) for i in range(RING_UP)]
        assert RING_DN == 6 and GROUP == 3
        arena = sb("arena", [128, RING_DN * D + 2 * GROUP * 512], BF16)
        wdn_r = [arena[:, i * D:(i + 1) * D] for i in range(RING_DN)]
        actb = [arena[:, RING_DN * D + i * 512:RING_DN * D + (i + 1) * 512] for i in range(2 * GROUP)]
        hT2 = arena[:, 0:4096].rearrange("p (c t) -> p c t", c=8)
        ycat2 = arena[:, 4096:8192].rearrange("p (c t) -> p c t", c=8)
        HT = [hT, hT2]
        YC = [ycat, ycat2]
        hbA = arena[:, 8192:9216]
        banks = [es.enter_context(nc.psum_tensor(f"bank{i}", [128, 512], F32)) for i in range(8)]

        pscale = small[:, 0:4]

        def cw(k, c):
            return small[:, 4 + k * 4 + c: 4 + k * 4 + c + 1]

        def fcw(k, ch):
            return small[:, 16 + k * 44 + ch: 16 + k * 44 + ch + 1]

        def invc(c):
            return small[:, 148 + c * 15: 148 + (c + 1) * 15]

        B_x1 = [Buf(f"x1_{i}") for i in range(9)]
        B_h2T = [Buf(f"h2T_{i}") for i in range(9)]
        B_g1, B_g2, B_gf, B_small, B_mhalf = Buf("g1"), Buf("g2"), Buf("gf"), Buf("small"), Buf("mhalf")
        B_st = [Buf(f"st{i}") for i in range(16)]
        B_wout, B_wpool, B_ident = Buf("wout"), Buf("wpool"), Buf("ident")
        B_identF = Buf("identF")
        B_winq = [Buf(f"win{q}") for q in range(4)]
        B_hb = [Buf("hb0"), Buf("hb1")]
        B_hT = [[Buf(f"hT{p_}_{i}") for i in range(4)] for p_ in range(2)]
        B_ycat = [[Buf(f"ycat{p_}_{i}") for i in range(8)] for p_ in range(2)]
        B_scr = [Buf(f"scr{i}") for i in range(NSCR)]
        B_dT = [Buf("dT0"), Buf("dT1")]
        B_Hu = [Buf(f"Hu{i}") for i in range(4)]
        B_Hv = [Buf(f"Hv{i}") for i in range(4)]
        B_Hf = [Buf(f"Hf{i}") for i in range(NPAIR)]
        B_ues = [Buf(f"ues{i}") for i in range(4)]
        B_ves = [Buf(f"ves{i}") for i in range(4)]
        B_hfs = [Buf(f"hfs{i}") for i in range(NPAIR)]
        B_wup = [[Buf(f"wup{i}a"), Buf(f"wup{i}b")] for i in range(RING_UP)]
        B_wdn = [Buf(f"wdn{i}") for i in range(RING_DN)]
        B_act = [Buf(f"act{i}") for i in range(2 * GROUP)]

        def alias(a_, b_):
            a_.aliases.append(b_)
            b_.aliases.append(a_)
        for t_ in range(4):
            for i_ in range(4):
                alias(B_hT[1][t_], B_wdn[i_])
        for e_ in range(8):
            alias(B_ycat[1][e_], B_wdn[4 + e_ // 2] if e_ < 4 else B_act[e_ - 4])
        B_hbA = Buf("hbA")
        alias(B_hbA, B_act[4])
        alias(B_hbA, B_act[5])
        hbB = ycat[:, 2:4, :].rearrange("p c t -> p (c t)")
        hbC = ycat[:, 4:6, :].rearrange("p c t -> p (c t)")
        B_hbB, B_hbC = Buf("hbB"), Buf("hbC")
        alias(B_hbB, B_ycat[0][2])
        alias(B_hbB, B_ycat[0][3])
        alias(B_hbC, B_ycat[0][4])
        alias(B_hbC, B_ycat[0][5])
        B_bank = [Buf(f"bank{i}") for i in range(8)]
        B_out = Buf("outs")

        cnt = {"bank": 0, "scr": 0, "hb": 0, "st": 0, "dT": 0, "act": 0, "bank_up": 0, "bank_dn": 0}

        def nxt(kind, n):
            v = cnt[kind] % n
            cnt[kind] += 1
            return v

        def nbank():
            k = nxt("bank", 8)
            return banks[k], B_bank[k]

        def nbank_up():
            k = nxt("bank_up", 4)
            return banks[k], B_bank[k]

        def nbank_dn():
            k = 4 + nxt("bank_dn", 4)
            return banks[k], B_bank[k]

        def nscr():
            k = nxt("scr", NSCR)
            return scr[k], B_scr[k]

        late = []

        def dma(eng, out, in_, reads, writes, name="", lag=0.0):
            P.op(eng, lambda e, out=out, in_=in_: e.dma_start(out=out, in_=in_), reads, writes, dma=True, name=name, lag=lag)

        def bcast(v):
            return v.rearrange("(o d) -> o d", o=1).to_broadcast([128, D])

        def load_x(part_tile, glob_tile, lag=0.0):
            dma("sp", x1[:, part_tile, :], x_d[glob_tile * 128:(glob_tile + 1) * 128, :], [], [B_x1[part_tile]],
                f"ld_x{glob_tile}", lag=lag)

        load_x(0, 0)
        dma("sp", g1r[:], bcast(g1_d), [], [B_g1], "ld_g1")
        for i in range(1, 4):
            load_x(i, i)
        dma("sp", small[:], small_d[:, :], [], [B_small], "ld_small")
        dma("sp", g2r[:], bcast(g2_d), [], [B_g2], "ld_g2")
        def late_loads():
            gate = [B_winq[1]]
            for i in range(4, 8):
                dma("sp", x1[:, i, :], x_d[i * 128:(i + 1) * 128, :], gate, [B_x1[i]], f"ld_x{i}")
            dma("sp", gfr[:], bcast(gf_d), gate, [B_gf], "ld_gf")
            dma("sp", identF[:], ident_d[:, :], gate, [B_identF], "ld_identF")
            dma("sp", ues[:, :, 0:15 * SPC], spT_d.rearrange("p (c q) -> p c q", c=4), gate, B_ues, "ld_sp")
            dma("sp", ves[:, :, 0:2 * SPC], scT_d.rearrange("p (c q) -> p c q", c=4), gate, B_ves, "ld_sc")
            dma("sp", hfs[:], sfT_d.rearrange("p (c q) -> p c q", c=44), gate, B_hfs, "ld_sf")

        dma("pool", ident[:], ident_d[:, :], [], [B_ident], "ld_ident")
        def load_win(q):
            gate_ = [B_x1[1]] if q == 0 else []
            dma("pool", w_in[:, :, q * 512:(q + 1) * 512],
                win_d.rearrange("(c p) e -> p c e", p=128)[:, :, q * 512:(q + 1) * 512], gate_, [B_winq[q]], f"ld_win{q}")

        def load_rest_of_mixer_weights():
            for q in (2, 3, 1):
                load_win(q)
            dma("pool", w_pool[:], wpool_d.rearrange("g c d -> c g d"), [], [B_wpool], "ld_wpool")
            dma("pool", w_out[:], wout_d.rearrange("(c p) e -> p c e", p=128), [], [B_wout], "ld_wout")
        load_win(0)

        P.op("pool", lambda e: e.memset(mhalf[:], -0.5), [], [B_mhalf], name="mhalf")
        P.op("pool", lambda e: e.memset(epsb[:], EPS), [], [B_mhalf], name="epsb")
        P.op("act", lambda e: e.activation(out=stt[:, 47:48], in_=epsb[:], func=AF.Sqrt), [B_mhalf], [Buf("warm")], name="warm")
        P.op("pool", lambda e: e.memset(Hu[:], 0.0), [], B_Hu, name="Hu0")
        P.op("pool", lambda e: e.memset(Hv[:], 0.0), [], B_Hv, name="Hv0")
        P.op("pool", lambda e: e.memset(Hf[:], 0.0), [], B_Hf, name="Hf0")

        up_slot = {}
        dn_slot = {}
        all_pairs = [(0, jj) for jj in range(NPAIR)] + [(1, jj) for jj in range(NPAIR)]
        pend_up = list(all_pairs)
        pend_dn = list(all_pairs)
        ring_n = {"up": 0, "dn": 0}

        def load_up(extra_reads=()):
            if not pend_up:
                return
            part, jj = pend_up.pop(0)
            s_ = ring_n["up"] % RING_UP
            ring_n["up"] += 1
            up_slot[(part, jj)] = s_
            for ab in range(2):
                f0 = ab * DFF + jj * 128
                src = wup_d.rearrange("(c p) f -> p c f", p=128)[:, :, f0:f0 + 128]
                dma("pool", wup_r[s_][:, ab, :, :], src, list(extra_reads), [B_wup[s_][ab]], f"ld_wup{part}_{jj}_{ab}")

        def load_dn(extra_reads=(), only_part=None):
            if not pend_dn:
                return
            if only_part is not None and pend_dn[0][0] != only_part:
                return
            part, jj = pend_dn.pop(0)
            s_ = ring_n["dn"] % RING_DN
            ring_n["dn"] += 1
            dn_slot[(part, jj)] = s_
            dma("pool", wdn_r[s_], wdn_d[jj * 128:(jj + 1) * 128, :], list(extra_reads), [B_wdn[s_]], f"ld_wdn{part}_{jj}")

        def rms_to_T(src_ap, B_src, grep, B_g, dstT, dst_cols, B_dst, lag=0.0, defer=False, fixed_hb=None):
            state = {}

            def front():
                if fixed_hb is None:
                    k = nxt("hb", 2)
                    hbk, Bhbk = hb[k], B_hb[k]
                else:
                    hbk, Bhbk = fixed_hb
                s = nxt("st", 16)
                c0 = s * 3
                state["k"] = (hbk, Bhbk)
                P.op("act", lambda e: e.activation(out=hbk[:, :], in_=src_ap, func=AF.Square, accum_out=stt[:, c0:c0 + 1]),
                     [B_src], [Bhbk, B_st[s]], name="sq", lag=lag, tiny=True)
                P.op("act", lambda e: e.activation(out=stt[:, c0 + 1:c0 + 2], in_=stt[:, c0:c0 + 1], func=AF.Sqrt,
                                                   scale=1.0 / D, bias=epsb[:]),
                     [B_st[s], B_mhalf], [B_st[s]], name="ms", lag=lag, tiny=True)
                P.op("dve", lambda e: e.reciprocal(out=stt[:, c0 + 2:c0 + 3], in_=stt[:, c0 + 1:c0 + 2]),
                     [B_st[s]], [B_st[s]], name="pow", lag=lag, tiny=True)
                P.op("dve", lambda e: e.scalar_tensor_tensor(out=hbk[:, :], in0=src_ap, scalar=stt[:, c0 + 2:c0 + 3], in1=grep[:],
                                                             op0=ALU.mult, op1=ALU.mult),
                     [B_src, B_st[s], B_g], [Bhbk], name="hnorm", lag=lag)

            def back():
                hbk, Bhbk = state["k"]
                bk, Bbk = nbank()
                pT = bk.bitcast(BF16)

                def tr(e):
                    ins = None
                    for c in range(8):
                        ins = e.transpose(out=pT[:, c * 128:(c + 1) * 128], in_=hbk[:, c * 128:(c + 1) * 128], identity=ident[:])
                    return ins
                P.op("pe", tr, [Bhbk, B_ident], [Bbk], name="tr", lag=lag)
                P.op("act", lambda e: e.activation(out=dstT[:, :, dst_cols[0]:dst_cols[1]],
                                                   in_=pT[:, :].rearrange("p (c t) -> p c t", c=8), func=AF.Copy),
                     [Bbk], [B_dst], name="trcp", lag=lag)
            if defer == "both":
                return front, back
            front()
            if defer:
                return back
            back()
            return None

        def stage_A(tiles):
            backs = []
            for t, i in enumerate(tiles):
                backs.append(rms_to_T(x1[:, i, :], B_x1[i], g1r, B_g1, HT[0], (t * 128, (t + 1) * 128), B_hT[0][t], defer=True))
                if len(backs) >= 2:
                    backs.pop(0)()
            for bfn in backs:
                bfn()

        def mixer_block(part, tiles, is_sample, first_prompt, next_tiles, par, next_par, before_nA=None, early_A=True):
            nt = len(tiles)
            NT = nt * 128
            hTb, ycb = HT[par], YC[par]
            Byc = B_ycat[par]
            hT_bufs = [B_hT[par][t] for t in range(nt)]

            def inproj(bk, echunk):
                def f(e):
                    ins = None
                    for d in range(8):
                        ins = e.matmul(bk[:, 0:NT], lhsT=w_in[:, d, echunk * 128:(echunk + 1) * 128], rhs=hTb[:, d, 0:NT],
                                       start=(d == 0), stop=(d == 7))
                    return ins
                return f

            TS = SPC if is_sample else 1
            HU = 15 * TS
            HV = 2 * TS

            def bu_front(c):
                win = WINS[c]
                bk, Bbk = nbank()
                P.op("pe", inproj(bk, c), hT_bufs + [B_winq[0]], [Bbk], name=f"inproj_u{c}")
                L = HU + NT
                if is_sample:
                    ue_flat = ues[:, c, :]
                    Bue = B_ues[c]
                else:
                    ue_flat, Bue = nscr()
                    P.op("dve", lambda e, su=ue_flat, c=c: e.tensor_copy(out=su[:, 0:15], in_=Hu[:, c, :]), [B_Hu[c]], [Bue],
                         name="halo_u", tiny=True)
                P.op("act", lambda e, bk=bk, su=ue_flat: e.activation(out=su[:, HU:L], in_=bk[:, 0:NT], func=AF.Copy),
                     [Bbk], [Bue], name="evac_u")
                tA, BtA = nscr()
                tB, BtB = nscr()
                cur, Bcur = ue_flat, Bue
                tmp = [(tA, BtA), (tB, BtB)]
                w = 1
                step = 0
                while w < win:
                    lo = (15 - (win - 2 * w)) * TS
                    sh = w * TS
                    dst, Bdst = tmp[step % 2]
                    P.op("dve", lambda e, dst=dst, cur=cur, lo=lo, sh=sh, L=L: e.tensor_tensor(
                        out=dst[:, lo:L], in0=cur[:, lo:L], in1=cur[:, lo - sh:L - sh], op=ALU.add),
                        [Bcur], [Bdst], name=f"pool_s{2 * w}")
                    cur, Bcur = dst, Bdst
                    w *= 2
                    step += 1
                kd = nxt("dT", 2)
                P.op("dve", lambda e, cur=cur, ue_flat=ue_flat, kd=kd, win=win: e.scalar_tensor_tensor(
                    out=dT[kd][:, 0:NT], in0=cur[:, HU:L], scalar=1.0 / win, in1=ue_flat[:, HU:L],
                    op0=ALU.mult, op1=ALU.subtract), [Bcur, Bue], [B_dT[kd]], name="pool_d")
                if first_prompt:
                    other, Bother = tmp[step % 2]
                    P.op("pool", lambda e, other=other, cur=cur, c=c: e.tensor_tensor(
                        out=other[:, 0:15], in0=cur[:, 15:30], in1=invc(c), op=ALU.mult),
                        [Bcur, B_small], [Bother], name="pool_fix1", tiny=True)
                    P.op("dve", lambda e, other=other, ue_flat=ue_flat, kd=kd: e.tensor_tensor(
                        out=dT[kd][:, 0:15], in0=other[:, 0:15], in1=ue_flat[:, 15:30], op=ALU.subtract),
                        [Bother, Bue], [B_dT[kd]], name="pool_fix2", tiny=True)
                if not is_sample:
                    P.op("dve", lambda e, ue_flat=ue_flat, c=c: e.tensor_copy(out=Hu[:, c, :], in_=ue_flat[:, NT:NT + 15]),
                         [Bue], [B_Hu[c]], name="halo_u_save", tiny=True)
                return kd

            def bu_back(c, kd):
                bk2, Bbk2 = nbank()
                P.op("pe", lambda e, bk2=bk2, kd=kd, c=c: e.matmul(bk2[:, 0:NT], lhsT=w_pool[:, c, :], rhs=dT[kd][:, 0:NT],
                                                                   start=True, stop=True),
                     [B_dT[kd], B_wpool], [Bbk2], name="poolmm")
                P.op("act", lambda e, bk2=bk2, c=c: e.activation(out=ycb[:, c, 0:NT], in_=bk2[:, 0:NT], func=AF.Identity,
                                                                 scale=pscale[:, c:c + 1]),
                     [Bbk2, B_small], [Byc[c]], name="evac_ypool")

            def bc(c):
                bgc, Bgc = nbank()
                bhv, Bhv = nbank()
                bgb, Bgb = nbank()
                P.op("pe", inproj(bgc, 8 + c), hT_bufs + [B_winq[2]], [Bgc], name=f"inproj_gc{c}")
                P.op("pe", inproj(bhv, 12 + c), hT_bufs + [B_winq[3]], [Bhv], name=f"inproj_hv{c}")
                P.op("pe", inproj(bgb, 4 + c), hT_bufs + [B_winq[1]], [Bgb], name=f"inproj_gb{c}")
                sg, Bsg = nscr()
                P.op("act", lambda e, sg=sg, bgc=bgc: e.activation(out=sg[:, 0:NT], in_=bgc[:, 0:NT], func=AF.Copy),
                     [Bgc], [Bsg], name="evac_gc")
                acc, Bacc = nscr()
                if is_sample:
                    sv = ves[:, c, :]
                    Bsv = B_ves[c]
                else:
                    sv, Bsv = nscr()
                    P.op("dve", lambda e, sv=sv, c=c: e.tensor_copy(out=sv[:, 0:2], in_=Hv[:, c, :]), [B_Hv[c]], [Bsv],
                         name="halo_v", tiny=True)
                P.op("dve", lambda e, sv=sv, bhv=bhv, sg=sg: e.tensor_tensor(
                    out=sv[:, HV:HV + NT], in0=bhv[:, 0:NT], in1=sg[:, 0:NT], op=ALU.mult), [Bhv, Bsg], [Bsv], name="v")
                P.op("dve", lambda e, sv=sv, acc=acc, c=c: e.tensor_scalar(
                    out=acc[:, 0:NT], in0=sv[:, HV:HV + NT], scalar1=cw(2, c), scalar2=None, op0=ALU.mult),
                    [Bsv, B_small], [Bacc], name="cv2")
                for k in (1, 0):
                    P.op("dve", lambda e, sv=sv, acc=acc, k=k, c=c: e.scalar_tensor_tensor(
                        out=acc[:, 0:NT], in0=sv[:, k * TS:k * TS + NT], scalar=cw(k, c), in1=acc[:, 0:NT], op0=ALU.mult, op1=ALU.add),
                        [Bsv, B_small, Bacc], [Bacc], name=f"cv{k}")
                if not is_sample:
                    P.op("dve", lambda e, sv=sv, c=c: e.tensor_copy(out=Hv[:, c, :], in_=sv[:, NT:NT + 2]),
                         [Bsv], [B_Hv[c]], name="halo_v_save", tiny=True)
                P.op("dve", lambda e, bgb=bgb, acc=acc, c=c: e.tensor_tensor(
                    out=ycb[:, 4 + c, 0:NT], in0=bgb[:, 0:NT], in1=acc[:, 0:NT], op=ALU.mult),
                    [Bgb, Bacc], [Byc[4 + c]], name="yconv")

            def genB():
                nA = []
                if next_tiles is not None:
                    nA = [rms_to_T(x1[:, i, :], B_x1[i], g1r, B_g1, HT[next_par], (t * 128, (t + 1) * 128), B_hT[next_par][t],
                                   defer="both", fixed_hb=(hbA, B_hbA) if early_A else None) for t, i in enumerate(next_tiles)]
                asteps = []
                if early_A:
                    for t in range(len(nA)):
                        asteps.append(nA[t][0])
                        asteps.append(nA[t][1])

                def a_step(n=1):
                    for _ in range(n):
                        if asteps:
                            asteps.pop(0)()
                k0 = bu_front(0)
                yield
                k1 = bu_front(1)
                a_step()
                yield
                bc(0)
                yield
                bu_back(0, k0)
                bu_back(1, k1)
                a_step()
                yield
                k2 = bu_front(2)
                a_step()
                yield
                k3 = bu_front(3)
                yield
                bc(1)
                a_step()
                yield
                bu_back(2, k2)
                bu_back(3, k3)
                a_step()
                yield
                if not early_A and nA:
                    for t in range(min(2, len(nA))):
                        nA[t][0]()
                bc(2)
                a_step()
                yield
                a_step()
                bc(3)
                a_step()
                yield
                a_step(8)
                if not early_A and nA:
                    nA[0][1]()
                    for t in range(1, len(nA)):
                        if t + 1 < len(nA):
                            nA[t + 1][0]()
                        nA[t][1]()

            def genC():
                f1s, f2s, f3s = [], [], []
                for t, i in enumerate(tiles):
                    def f1(t=t, i=i):
                        halves = []
                        for hh in range(2):
                            bo, Bbo = nbank()
                            halves.append((bo, Bbo))

                            def f(e, bo=bo, hh=hh, t=t):
                                ins = None
                                for ec in range(8):
                                    ins = e.matmul(bo[:, :], lhsT=ycb[:, ec, t * 128:(t + 1) * 128], rhs=w_out[:, ec, hh * 512:(hh + 1) * 512],
                                                   start=(ec == 0), stop=(ec == 7))
                                return ins
                            P.op("pe", f, Byc + [B_wout], [Bbo], name="outproj")
                        for hh in range(2):
                            bo, Bbo = halves[hh]
                            P.op("dve", lambda e, bo=bo, hh=hh, i=i: e.tensor_tensor(
                                out=x1[:, i, hh * 512:(hh + 1) * 512], in0=x1[:, i, hh * 512:(hh + 1) * 512], in1=bo[:, :], op=ALU.add),
                                [Bbo, B_x1[i]], [B_x1[i]], name="resid1")
                    fr, bk_ = rms_to_T(x1[:, i, :], B_x1[i], g2r, B_g2, h2T, (i * 128, (i + 1) * 128), B_h2T[i], defer="both")
                    f1s.append(f1)
                    f2s.append(fr)
                    f3s.append(bk_)
                n_t = len(tiles)
                for step in range(n_t + 3):
                    if 0 <= step - 3 < n_t:
                        f3s[step - 3]()
                    if step < n_t:
                        f1s[step]()
                    if 0 <= step - 1 < n_t:
                        f2s[step - 1]()
                    yield

            return genB(), genC()

        unit_backs = []
        UNIT_LAG = 0
        reload_after_store = {(0, i): 8 + i for i in range(4)}

        def flush_units():
            while unit_backs:
                unit_backs.pop(0)()

        def ffn_unit(part, jj, tiles, is_sample, reload_up):
            nt = len(tiles)
            NT = nt * 128
            c0 = tiles[0] * 128
            s = up_slot[(part, jj)]
            h2bufs = [B_h2T[i] for i in tiles]
            accs = []
            for ab in range(2):
                bk, Bbk = nbank_up()

                def f(e, bk=bk, ab=ab):
                    ins = None
                    for d in range(8):
                        ins = e.matmul(bk[:, 0:NT], lhsT=wup_r[s][:, ab, d, :], rhs=h2T[:, d, c0:c0 + NT], start=(d == 0), stop=(d == 7))
                    return ins
                P.op("pe", f, h2bufs + [B_wup[s][ab]], [Bbk], name=f"up{ab}")
                ch = ab * NPAIR + jj
                sx, Bsx = nscr()
                acc, Bacc = nscr()
                TS = SPC if is_sample else 1
                HF = 2 * TS
                if is_sample:
                    P.op("pool", lambda e, sx=sx, ch=ch: e.tensor_copy(out=sx[:, 0:HF], in_=hfs[:, ch, :]),
                         [B_hfs[jj]], [Bsx], name="hist_f", tiny=True)
                else:
                    P.op("act", lambda e, sx=sx, ab=ab: e.activation(out=sx[:, 0:2], in_=Hf[:, jj, ab, :], func=AF.Copy),
                         [B_Hf[jj]], [Bsx], name="halo_f", tiny=True)
                P.op("act", lambda e, sx=sx, bk=bk: e.activation(out=sx[:, HF:HF + NT], in_=bk[:, 0:NT], func=AF.Copy),
                     [Bbk], [Bsx], name="evac_up")
                P.op("act", lambda e, acc=acc, bk=bk, ch=ch: e.activation(out=acc[:, 0:NT], in_=bk[:, 0:NT], func=AF.Identity,
                                                                         scale=fcw(2, ch)),
                     [Bbk, B_small], [Bacc], name="evac_up_s")
                for k in (1, 0):
                    P.op("dve", lambda e, acc=acc, sx=sx, k=k, ch=ch: e.scalar_tensor_tensor(
                        out=acc[:, 0:NT], in0=sx[:, k * TS:k * TS + NT], scalar=fcw(k, ch), in1=acc[:, 0:NT], op0=ALU.mult, op1=ALU.add),
                        [Bsx, B_small, Bacc], [Bacc], name=f"fcv{k}")
                if is_sample:
                    P.op("pool", lambda e, sx=sx, ch=ch: e.tensor_copy(out=hfs[:, ch, :], in_=sx[:, NT:NT + HF]),
                         [Bsx], [B_hfs[jj]], name="hist_f_save", tiny=True)
                else:
                    P.op("act", lambda e, sx=sx, ab=ab: e.activation(out=Hf[:, jj, ab, :], in_=sx[:, NT:NT + 2], func=AF.Copy),
                         [Bsx], [B_Hf[jj]], name="halo_f_save", tiny=True)
                accs.append((acc, Bacc))
            if reload_up:
                load_up()
            (acc_a, Bacc_a), (acc_b, Bacc_b) = accs
            ka = nxt("act", 2 * GROUP)

            def back():
                sa, Bsa = nscr()
                P.op("act", lambda e, sa=sa, acc_a=acc_a: e.activation(out=sa[:, 0:NT], in_=acc_a[:, 0:NT], func=AF.Silu),
                     [Bacc_a], [Bsa], name="silu")
                P.op("dve", lambda e, sa=sa, acc_b=acc_b, ka=ka: e.tensor_tensor(
                    out=actb[ka][:, 0:NT], in0=sa[:, 0:NT], in1=acc_b[:, 0:NT], op=ALU.mult), [Bsa, Bacc_b], [B_act[ka]], name="gate")
            unit_backs.append(back)
            while len(unit_backs) > UNIT_LAG:
                unit_backs.pop(0)()
            return ka

        wide_state = [0]

        def ffn_down(part, group, kas, tiles, last_group, glob_tile0, only=None, wide=False):
            wide_seq = [4, 5, 6, 7, 0, 1, 2, 3]
            wide_n = wide_state

            def nbank_flush():
                k = wide_seq[wide_n[0] % 8]
                wide_n[0] += 1
                return banks[k], B_bank[k]
            for t, i in enumerate(tiles):
                if only is not None and t not in only:
                    continue
                halves = []
                for hh in range(2):
                    bo, Bbo = nbank_flush() if wide else nbank_dn()
                    halves.append((bo, Bbo))

                    def f(e, bo=bo, hh=hh, t=t, i=i):
                        ins = None
                        if last_group:
                            ins = e.matmul(bo[:, :], lhsT=identF[:, :], rhs=x1[:, i, hh * 512:(hh + 1) * 512], start=True, stop=False)
                        for q, jj in enumerate(group):
                            s = dn_slot[(part, jj)]
                            ins = e.matmul(bo[:, :], lhsT=actb[kas[q]][:, t * 128:(t + 1) * 128], rhs=wdn_r[s][:, hh * 512:(hh + 1) * 512],
                                           start=(q == 0 and not last_group), stop=(q == len(group) - 1))
                        return ins
                    rds = [B_act[k] for k in kas] + [B_wdn[dn_slot[(part, jj)]] for jj in group]
                    if last_group:
                        rds = rds + [B_x1[i], B_identF]
                    P.op("pe", f, rds, [Bbo], name="down")
                if not last_group:
                    for hh in range(2):
                        bo, Bbo = halves[hh]
                        P.op("dve", lambda e, bo=bo, hh=hh, i=i: e.tensor_tensor(
                            out=x1[:, i, hh * 512:(hh + 1) * 512], in0=x1[:, i, hh * 512:(hh + 1) * 512], in1=bo[:, :], op=ALU.add),
                            [Bbo, B_x1[i]], [B_x1[i]], name="resid2")
                else:
                    s = nxt("st", 16)
                    c0 = s * 3
                    for hh in range(2):
                        bo, Bbo = halves[hh]
                        P.op("act", lambda e, bo=bo, hh=hh, c0=c0: e.activation(out=ycat[:, hh, :], in_=bo[:, :], func=AF.Square,
                                                                               accum_out=stt[:, c0 + hh:c0 + hh + 1]),
                             [Bbo], [B_ycat[0][hh], B_st[s]], name="fsq")
                    P.op("pool", lambda e, c0=c0: e.tensor_tensor(out=stt[:, c0:c0 + 1], in0=stt[:, c0:c0 + 1], in1=stt[:, c0 + 1:c0 + 2],
                                                                  op=ALU.add), [B_st[s]], [B_st[s]], name="fss", tiny=True)
                    P.op("pool", lambda e, c0=c0: e.tensor_scalar(out=stt[:, c0 + 1:c0 + 2], in0=stt[:, c0:c0 + 1], scalar1=1.0 / D,
                                                                  scalar2=EPS, op0=ALU.mult, op1=ALU.add), [B_st[s]], [B_st[s]], name="fms", tiny=True)
                    P.op("pool", lambda e, c0=c0: e.tensor_tensor(out=stt[:, c0 + 2:c0 + 3], in0=stt[:, c0 + 1:c0 + 2], in1=mhalf[:],
                                                                  op=ALU.pow), [B_st[s], B_mhalf], [B_st[s]], name="fpow", tiny=True)
                    gt = glob_tile0 + t
                    src = x1[:, i, :]

                    def fin(i=i, src=src, c0=c0, s=s, gt=gt, halves=halves):
                        for hh in range(2):
                            bo, Bbo = halves[hh]
                            P.op("dve", lambda e, bo=bo, hh=hh, c0=c0, i=i: e.scalar_tensor_tensor(
                                out=x1[:, i, hh * 512:(hh + 1) * 512], in0=bo[:, :], scalar=stt[:, c0 + 2:c0 + 3],
                                in1=gfr[:, hh * 512:(hh + 1) * 512], op0=ALU.mult, op1=ALU.mult),
                                [Bbo, B_st[s], B_gf], [B_x1[i]], name="fnorm")
                        dma("sp", y_d[gt * 128:(gt + 1) * 128, :], src, [B_x1[i]], [B_out], f"st_y{gt}")
                        if (part, i) in reload_after_store:
                            if i >= 1:
                                load_x(i - 1, reload_after_store[(part, i - 1)])
                            if i == 3:
                                load_x(3, reload_after_store[(part, 3)])
                    final_pending.append(fin)
                    while len(final_pending) > (0 if wide else 1):
                        final_pending.pop(0)()

        final_pending = []

        def flush_final():
            while final_pending:
                final_pending.pop(0)()

        gsz = [2, 2] + [GROUP] * ((NPAIR - 4) // GROUP)
        assert sum(gsz) == NPAIR
        groups = []
        for z in gsz:
            groups.append(list(range(sum(len(g_) for g_ in groups), sum(len(g_) for g_ in groups) + z)))
        parts = [
            [([0, 1, 2, 3], False, True, 0), ([4, 5, 6, 7], False, False, 4)],
            [([0, 1, 2, 3], False, False, 8), ([4, 5, 6, 7], False, False, 12), ([8], True, False, 16)],
        ]
        for part, blocks in enumerate(parts):
            if part == 1:
                for i in range(4, 9):
                    load_x(i, 8 + i)
            if part == 0:
                stage_A(blocks[0][0])
            if part == 0:
                load_rest_of_mixer_weights()
                late_loads()
            prevC = None
            for bi, (tiles, is_sample, first_prompt, gt0) in enumerate(blocks):
                nxt_tiles = blocks[bi + 1][0] if bi + 1 < len(blocks) else None
                par = bi % 2

                def finish_prev(pc=prevC):
                    if pc is not None:
                        for _ in pc:
                            pass
                gB, gC = mixer_block(part, tiles, is_sample, first_prompt, nxt_tiles, par, 1 - par, before_nA=finish_prev,
                                     early_A=not (part == 0 and bi == 0))
                for _ in gB:
                    if prevC is not None:
                        next(prevC, None)
                finish_prev()
                prevC = gC
                if part == 0 and bi == 0:
                    for _ in range(RING_UP):
                        load_up(extra_reads=[B_wout])
            for _ in prevC:
                pass
            for _ in range(RING_DN):
                load_dn(only_part=part)
            if part == 1:
                dma("sp", npp_d.rearrange("p (c k) -> p c k", c=4), Hu[:], B_Hu, [B_out], "st_npp")
                dma("sp", ncp_d.rearrange("p (c k) -> p c k", c=4), Hv[:], B_Hv, [B_out], "st_ncp")
                dma("sp", nps_d.rearrange("p (c q) -> p c q", c=4), ues[:, :, 8 * SPC:23 * SPC], B_ues, [B_out], "st_nps")
                dma("sp", ncs_d.rearrange("p (c q) -> p c q", c=4), ves[:, :, 8 * SPC:10 * SPC], B_ves, [B_out], "st_ncs")
            fblocks = [b_ for b_ in blocks if b_[1]] + [b_ for b_ in blocks if not b_[1]]
            prev = None
            for gi, group in enumerate(groups):
                last_group = gi == len(groups) - 1
                for bi, (tiles, is_sample, first_prompt, gt0) in enumerate(fblocks):
                    last_blk = bi == len(fblocks) - 1
                    kas = []
                    todo = list(range(len(prev[0][2]))) if prev is not None else []
                    nun = len(group)
                    for q, jj in enumerate(group):
                        kas.append(ffn_unit(part, jj, tiles, is_sample, last_blk))
                        if prev is not None:
                            n_now = (len(prev[0][2]) * (q + 1)) // nun - (len(prev[0][2]) * q) // nun
                            if q == nun - 1:
                                n_now = len(todo)
                            if last_group and part + 1 < len(parts) and last_blk:
                                nT_ = len(prev[0][2])
                                n_now = min(len(todo), -((-nT_ * (q + 1)) // nun) + ((-nT_ * q) // nun))
                                if q == nun - 1:
                                    n_now = len(todo)
                            sel = todo[:n_now]
                            todo = todo[n_now:]
                            if sel:
                                ffn_down(part, *prev[0], only=sel)
                                if last_group and part + 1 < len(parts) and last_blk and not todo:
                                    flush_final()
                    if prev is not None and prev[1]:
                        for _ in prev[0][0]:
                            load_dn(only_part=part)
                    prev = ((group, kas, tiles, last_group, gt0), last_blk)
            flush_units()
            nxtA = None
            if part + 1 < len(parts):
                nb = parts[part + 1][0]
                xhb = {2: (hbB, B_hbB), 3: (hbC, B_hbC)}
                nxtA = [rms_to_T(x1[:, i, :], B_x1[i], g1r, B_g1, HT[0], (t * 128, (t + 1) * 128), B_hT[0][t], defer="both",
                                 fixed_hb=xhb.get(t)) for t, i in enumerate(nb[0])]
            if nxtA is not None:
                flush_final()
                nxtA[0][0]()
                nxtA[1][0]()
            wide_state[0] = 0
            if nxtA is not None and len(prev[0][2]) == 4:
                ffn_down(part, *prev[0], only=[0, 1], wide=True)
                for t in range(2, len(nxtA)):
                    nxtA[t][0]()
                ffn_down(part, *prev[0], only=[2, 3], wide=True)
            else:
                ffn_down(part, *prev[0], wide=True)
                if nxtA is not None:
                    for t in range(2, len(nxtA)):
                        nxtA[t][0]()
            flush_final()
            for _ in prev[0][0]:
                load_dn(only_part=part)
            if nxtA is not None:
                cnt["bank"] = 4
                for t in range(len(nxtA)):
                    nxtA[t][1]()
        dma("sp", nfp_d.rearrange("p (j a k) -> p j a k", j=NPAIR, a=2), Hf[:], B_Hf, [B_out], "st_nfp")
        dma("sp", nfs_d.rearrange("p (c q) -> p c q", c=44), hfs[:], B_hfs, [B_out], "st_nfs")

        DMA_K = {"sp": 8, "pool": 32, "act": 1, "pe": 1, "dve": 1}
        streams = P.schedule(DMA_K)
        sem_eng = {e: es.enter_context(nc.semaphore(f"s_{e}")) for e in ("pe", "act", "dve", "pool")}
        sem_dma = {e: [es.enter_context(nc.semaphore(f"d_{e}{i}")) for i in range(DMA_K[e])] for e in ("sp", "pool")}
        block = es.enter_context(nc.Block())

        def emit(e, name):
            for o in streams[name]:
                for d in o.waits:
                    if d.is_dma:
                        e.wait_ge(sem_dma[d.eng][d.dma_sem], d.dma_val)
                    else:
                        e.wait_ge(sem_eng[d.eng], d.sigidx)
                ins = o.fn(e)
                if o.is_dma:
                    ins.then_inc(sem_dma[name][o.dma_sem], 16)
                elif o.signal:
                    ins.then_inc(sem_eng[name], 1)
            h = P.dma_hist[name]
            for o in h[-DMA_K[name]:]:
                e.wait_ge(sem_dma[name][o.dma_sem], o.dma_val)

        @block.sync
        def _(e):
            emit(e, "sp")

        @block.gpsimd
        def _(e):
            emit(e, "pool")

        @block.scalar
        def _(e):
            emit(e, "act")

        @block.vector
        def _(e):
            emit(e, "dve")

        @block.tensor
        def _(e):
            emit(e, "pe")
    return nc


_CACHE = {}


def _small_params(pool_scale, conv_w, ffn_conv_w):
    sm = np.zeros((128, 208), np.float32)
    sm[:, 0:4] = pool_scale.reshape(4, 128).T
    for k in range(3):
        sm[:, 4 + k * 4:4 + (k + 1) * 4] = conv_w[k].reshape(4, 128).T
        sm[:, 16 + k * 44:16 + (k + 1) * 44] = ffn_conv_w[k].reshape(44, 128).T
    for c, win in enumerate(WINS):
        for t in range(15):
            sm[:, 148 + c * 15 + t] = np.float32(1.0) / np.float32(min(t + 1, win))
    return sm


def _fm(a, nchunk):
    S, K, C = a.shape
    t = a.reshape(S, K, nchunk, 128).transpose(3, 2, 1, 0)
    return np.ascontiguousarray(t).reshape(128, nchunk * S * K)


def _unfm(a, nchunk, S, K):
    t = a.reshape(128, nchunk, K, S).transpose(3, 2, 1, 0)
    return np.ascontiguousarray(t).reshape(S, K, nchunk * 128)


def kernel(x_prompt, x_sample, state_pool, state_conv, state_ffn, norm1_g, w_in, w_pool_grp, pool_scale, conv_w,
           w_out, norm2_g, w_up, ffn_conv_w, w_down, final_g):
    f = lambda a: np.ascontiguousarray(np.asarray(a, dtype=np.float32))
    x_prompt, x_sample = f(x_prompt), f(x_sample)
    state_pool, state_conv, state_ffn = f(state_pool), f(state_conv), f(state_ffn)
    if "nc" not in _CACHE:
        _CACHE["nc"] = build_program()
    nc = _CACHE["nc"]
    small = _small_params(f(pool_scale)[0], f(conv_w)[0], f(ffn_conv_w)[0])
    shared = {
        "g1": f(norm1_g)[0], "g2": f(norm2_g)[0], "gf": f(final_g),
        "w_in": f(w_in)[0], "w_pool": f(w_pool_grp)[0], "w_out": f(w_out)[0],
        "w_up": f(w_up)[0], "w_down": f(w_down)[0], "small": small,
        "ident": np.eye(128, dtype=np.float32),
    }
    in_maps = []
    for c in range(NCORES):
        xs = x_sample[c * SPC:(c + 1) * SPC].transpose(1, 0, 2).reshape(SPC * DEC_T, D)
        m = dict(shared)
        m["x"] = np.ascontiguousarray(np.concatenate([x_prompt[c], xs], axis=0))
        m["spT"] = _fm(state_pool[0, c * SPC:(c + 1) * SPC], 4)
        m["scT"] = _fm(state_conv[0, c * SPC:(c + 1) * SPC], 4)
        m["sfT"] = _fm(state_ffn[0, c * SPC:(c + 1) * SPC], 44)
        in_maps.append(m)
    res = run_bass_kernel_spmd(nc, in_maps, core_ids=list(range(NCORES)))
    R = res.results
    y_prompt = np.stack([R[c]["y"][:SEQ] for c in range(NCORES)], axis=0)
    y_sample = np.concatenate([R[c]["y"][SEQ:].reshape(DEC_T, SPC, D).transpose(1, 0, 2) for c in range(NCORES)], axis=0)
    npp = np.stack([_unfm(R[c]["npp"], 4, 1, 15)[0] for c in range(NCORES)], axis=0)[None]
    ncp = np.stack([_unfm(R[c]["ncp"], 4, 1, 2)[0] for c in range(NCORES)], axis=0)[None]
    nfp_l = []
    for c in range(NCORES):
        a = R[c]["nfp"].reshape(128, NPAIR, 2, 2).transpose(0, 2, 1, 3)
        a = np.ascontiguousarray(a).reshape(128, 44 * 1 * 2)
        nfp_l.append(_unfm(a, 44, 1, 2)[0])
    nfp = np.stack(nfp_l, axis=0)[None]
    nps = np.concatenate([_unfm(R[c]["nps"], 4, SPC, 15) for c in range(NCORES)], axis=0)[None]
    ncs = np.concatenate([_unfm(R[c]["ncs"], 4, SPC, 2) for c in range(NCORES)], axis=0)[None]
    nfs = np.concatenate([_unfm(R[c]["nfs"], 44, SPC, 2) for c in range(NCORES)], axis=0)[None]
    return (y_prompt, y_sample, npp, ncp, nfp, nps, ncs, nfs)
```
